# Optimizing a Trainium2 kernel written in Bass

```python
import math
import jax, jax.numpy as jnp
from jax import lax
import numpy as np

D_MODEL = 1024
BATCH = 32
SEQ = 2048
DEPTH = 1

D_FF = 2816
SSM_EXPAND = 2
SSM_D_INNER = SSM_EXPAND * D_MODEL
SSM_HEAD_DIM = 64
SSM_HEADS = SSM_D_INNER // SSM_HEAD_DIM
SSM_GROUPS = 8
SSM_HEADS_PER_GROUP = SSM_HEADS // SSM_GROUPS
SSM_STATE = 128
SSM_CONV = 4
SSM_CHUNK = 128
SSM_CONV_DIM = SSM_D_INNER + 2 * SSM_GROUPS * SSM_STATE
ATT_HEADS = 8
ATT_HEAD_DIM = 128
ATT_WIDTH = ATT_HEADS * ATT_HEAD_DIM
MOBA_BLOCK = 256
MOBA_TOPK = 3
Q_BLOCK = 128
N_BRANCHES = 2
IN_COLS = SSM_D_INNER + SSM_CONV_DIM + SSM_HEADS + 3 * ATT_WIDTH + N_BRANCHES * D_MODEL
DEEPNORM_ALPHA = (2 * DEPTH) ** 0.25
DEEPNORM_BETA = (8 * DEPTH) ** -0.25
LN_EPS = 1e-5
RMS_EPS = 1e-5

kernel_name = "hybrid_ssd_moba_macaron_deepnorm"


def layer_norm(x, g, b):
    xf = x.astype(jnp.float32)
    mu = xf.mean(-1, keepdims=True)
    var = jnp.square(xf - mu).mean(-1, keepdims=True)
    y = (xf - mu) * lax.rsqrt(var + LN_EPS) * g.astype(jnp.float32) + b.astype(jnp.float32)
    return y.astype(x.dtype)


def swiglu(x, w_gate, w_up, w_down):
    return (jax.nn.silu(x @ w_gate) * (x @ w_up)) @ w_down


def causal_depthwise_conv(u, w, b):
    out = lax.conv_general_dilated(
        u, w[:, None, :].astype(u.dtype), window_strides=(1,), padding=[(SSM_CONV - 1, 0)],
        dimension_numbers=("NWC", "WIO", "NWC"), feature_group_count=u.shape[-1])
    return out + b


def ssd_chunked_scan(xh, dt, a, bm, cm):
    bsz, s, g, r, p = xh.shape
    n = bm.shape[-1]
    nc = s // SSM_CHUNK

    def to_chunks(t):
        return jnp.moveaxis(t.reshape(bsz, nc, SSM_CHUNK, *t.shape[2:]), 1, 0)

    xs = (to_chunks(xh * dt[..., None]), to_chunks(dt * a), to_chunks(bm), to_chunks(cm))
    causal = jnp.tril(jnp.ones((SSM_CHUNK, SSM_CHUNK), dtype=bool))

    def step(state, inp):
        xdt, da, b, c = inp
        acum = jnp.moveaxis(jnp.cumsum(da, axis=1), 1, -1)
        seg = acum[..., :, None] - acum[..., None, :]
        decay_ls = jnp.exp(jnp.where(causal, seg, -jnp.inf))
        cb = jnp.einsum("blgn,bsgn->bgls", c, b)
        y = jnp.einsum("bgrls,bsgrp->blgrp", decay_ls * cb[:, :, None], xdt)
        y = y + jnp.einsum("blgn,bgrpn,bgrl->blgrp", c, state, jnp.exp(acum))
        last = acum[..., -1:]
        new_state = state * jnp.exp(last)[..., None] + jnp.einsum(
            "bsgn,bgrs,bsgrp->bgrpn", b, jnp.exp(last - acum), xdt)
        return new_state, y

    state0 = jnp.zeros((bsz, g, r, p, n), jnp.float32)
    _, ys = lax.scan(step, state0, xs)
    return jnp.moveaxis(ys, 0, 1).reshape(bsz, s, g, r, p)


def mamba2_branch(z, xbc, dt_raw, conv_w, conv_b, dt_bias, a_log, d_skip, norm_w):
    bsz, s, _ = z.shape
    g, r, p, n = SSM_GROUPS, SSM_HEADS_PER_GROUP, SSM_HEAD_DIM, SSM_STATE
    xbc = jax.nn.silu(causal_depthwise_conv(xbc, conv_w, conv_b))
    xs, bm, cm = jnp.split(xbc, [SSM_D_INNER, SSM_D_INNER + g * n], axis=-1)
    xh = xs.reshape(bsz, s, g, r, p).astype(jnp.float32)
    bm = bm.reshape(bsz, s, g, n).astype(jnp.float32)
    cm = cm.reshape(bsz, s, g, n).astype(jnp.float32)
    dt = jax.nn.softplus(dt_raw.astype(jnp.float32) + dt_bias.astype(jnp.float32)).reshape(bsz, s, g, r)
    a = -jnp.exp(a_log.astype(jnp.float32)).reshape(g, r)
    y = ssd_chunked_scan(xh, dt, a, bm, cm)
    y = y + d_skip.astype(jnp.float32).reshape(g, r)[:, :, None] * xh
    y = y.reshape(bsz, s, SSM_D_INNER) * jax.nn.silu(z.astype(jnp.float32))
    yg = y.reshape(bsz, s, SSM_GROUPS, -1)
    yg = yg * lax.rsqrt(jnp.mean(jnp.square(yg), -1, keepdims=True) + RMS_EPS)
    return (yg.reshape(bsz, s, SSM_D_INNER) * norm_w.astype(jnp.float32)).astype(z.dtype)


def moba_attention(q, k, v):
    bsz, s, h, dh = q.shape
    nb = -(-s // MOBA_BLOCK)
    s_pad = nb * MOBA_BLOCK
    nqb = s // Q_BLOCK
    topk = min(MOBA_TOPK, nb)
    scale = 1.0 / math.sqrt(dh)
    q = q.transpose(0, 2, 1, 3)
    k = k.transpose(0, 2, 1, 3)
    v = v.transpose(0, 2, 1, 3)
    pad = ((0, 0), (0, 0), (0, s_pad - s), (0, 0))
    k_blocks = jnp.pad(k, pad).reshape(bsz, h, nb, MOBA_BLOCK, dh)
    v_blocks = jnp.pad(v, pad).reshape(bsz, h, nb, MOBA_BLOCK, dh)
    k_mean = k_blocks.astype(jnp.float32).mean(axis=3)
    gate = jnp.einsum("bhsd,bhjd->bhsj", q.astype(jnp.float32), k_mean)
    q_blk = jnp.arange(s) // MOBA_BLOCK
    fully_past = jnp.arange(nb)[None, :] < q_blk[:, None]
    gate = jnp.where(fully_past, gate, -jnp.inf)
    _, sel = lax.top_k(gate, topk)
    valid = sel < q_blk[:, None]

    def to_qblocks(t):
        c = t.shape[-1]
        return t.reshape(bsz, h, nqb, Q_BLOCK, c).transpose(0, 2, 1, 3, 4).reshape(bsz * nqb, h, Q_BLOCK, c)

    b_idx = jnp.repeat(jnp.arange(bsz), nqb)
    qb_idx = jnp.tile(jnp.arange(nqb), bsz)
    head_idx = jnp.arange(h)[:, None, None]
    offs_q = jnp.arange(Q_BLOCK)
    offs_k = jnp.arange(MOBA_BLOCK)

    def one_block(inp):
        b, qi, qb, sb, ok = inp
        kb = k_blocks[b]
        vb = v_blocks[b]
        own = (qi * Q_BLOCK) // MOBA_BLOCK
        k_own = lax.dynamic_index_in_dim(kb, own, axis=1, keepdims=False)
        v_own = lax.dynamic_index_in_dim(vb, own, axis=1, keepdims=False)
        k_sel = kb[head_idx, sb]
        v_sel = vb[head_idx, sb]
        s_own = jnp.einsum("hqd,hkd->hqk", qb, k_own).astype(jnp.float32) * scale
        causal = (own * MOBA_BLOCK + offs_k)[None, :] <= (qi * Q_BLOCK + offs_q)[:, None]
        s_own = jnp.where(causal, s_own, -jnp.inf)
        s_sel = jnp.einsum("hqd,hqjkd->hqjk", qb, k_sel).astype(jnp.float32) * scale
        s_sel = jnp.where(ok[..., None], s_sel, -jnp.inf)
        scores = jnp.concatenate([s_own, s_sel.reshape(h, Q_BLOCK, topk * MOBA_BLOCK)], axis=-1)
        probs = jax.nn.softmax(scores, axis=-1).astype(v.dtype)
        p_own = probs[..., :MOBA_BLOCK]
        p_sel = probs[..., MOBA_BLOCK:].reshape(h, Q_BLOCK, topk, MOBA_BLOCK)
        return (jnp.einsum("hqk,hkd->hqd", p_own, v_own)
                + jnp.einsum("hqjk,hqjkd->hqd", p_sel, v_sel))

    out = lax.map(one_block, (b_idx, qb_idx, to_qblocks(q), to_qblocks(sel), to_qblocks(valid)))
    return out.reshape(bsz, nqb, h, Q_BLOCK, dh).transpose(0, 1, 3, 2, 4).reshape(bsz, s, h * dh)


def hybrid_mixer(h, w_in, conv_w, conv_b, dt_bias, a_log, d_skip, ssm_norm_w,
                 w_ssm_out, w_att_out, b_gate, w_o):
    bsz, s, _ = h.shape
    proj = h @ w_in
    splits = np.cumsum([SSM_D_INNER, SSM_CONV_DIM, SSM_HEADS, ATT_WIDTH, ATT_WIDTH, ATT_WIDTH]).tolist()
    z, xbc, dt_raw, q, k, v, g_raw = jnp.split(proj, splits, axis=-1)
    y_ssm = mamba2_branch(z, xbc, dt_raw, conv_w, conv_b, dt_bias, a_log, d_skip, ssm_norm_w) @ w_ssm_out
    shp = (bsz, s, ATT_HEADS, ATT_HEAD_DIM)
    y_att = moba_attention(q.reshape(shp), k.reshape(shp), v.reshape(shp)) @ w_att_out
    gates = jax.nn.sigmoid((g_raw + b_gate).astype(jnp.float32)).astype(h.dtype)
    g_ssm, g_att = jnp.split(gates, N_BRANCHES, axis=-1)
    return (g_ssm * y_ssm + g_att * y_att) @ w_o


def setup_inputs(seed: int = 0) -> dict:
    key = jax.random.key(seed)
    ks = jax.random.split(key, 32)
    f32 = jnp.float32

    def nrm(k, shape, scale):
        return jax.random.normal(k, shape, f32) * scale

    def gain(k, n):
        return 1.0 + 0.01 * jax.random.normal(k, (n,), f32)

    dt0 = jnp.exp(jax.random.uniform(ks[10], (SSM_HEADS,), f32)
                  * (math.log(0.1) - math.log(0.001)) + math.log(0.001))
    dt0 = jnp.maximum(dt0, 1e-4)
    return {
        "x": jax.random.normal(ks[0], (BATCH, SEQ, D_MODEL), f32),
        "ffn1_w_gate": nrm(ks[1], (D_MODEL, D_FF), D_MODEL ** -0.5),
        "ffn1_w_up": nrm(ks[2], (D_MODEL, D_FF), D_MODEL ** -0.5),
        "ffn1_w_down": nrm(ks[3], (D_FF, D_MODEL), DEEPNORM_BETA * D_FF ** -0.5),
        "ln1_g": gain(ks[4], D_MODEL),
        "ln1_b": nrm(ks[5], (D_MODEL,), 0.01),
        "w_in": nrm(ks[6], (D_MODEL, IN_COLS), D_MODEL ** -0.5),
        "conv_w": nrm(ks[7], (SSM_CONV, SSM_CONV_DIM), SSM_CONV ** -0.5),
        "conv_b": nrm(ks[8], (SSM_CONV_DIM,), 0.01),
        "dt_bias": dt0 + jnp.log(-jnp.expm1(-dt0)),
        "a_log": jnp.log(jax.random.uniform(ks[11], (SSM_HEADS,), f32, 1.0, 16.0)),
        "d_skip": gain(ks[12], SSM_HEADS),
        "ssm_norm_w": gain(ks[13], SSM_D_INNER),
        "w_ssm_out": nrm(ks[14], (SSM_D_INNER, D_MODEL), SSM_D_INNER ** -0.5),
        "w_att_out": nrm(ks[15], (ATT_WIDTH, D_MODEL), ATT_WIDTH ** -0.5),
        "b_gate": nrm(ks[16], (N_BRANCHES * D_MODEL,), 0.01),
        "w_o": nrm(ks[17], (D_MODEL, D_MODEL), DEEPNORM_BETA * D_MODEL ** -0.5),
        "ln2_g": gain(ks[18], D_MODEL),
        "ln2_b": nrm(ks[19], (D_MODEL,), 0.01),
        "ffn2_w_gate": nrm(ks[20], (D_MODEL, D_FF), D_MODEL ** -0.5),
        "ffn2_w_up": nrm(ks[21], (D_MODEL, D_FF), D_MODEL ** -0.5),
        "ffn2_w_down": nrm(ks[22], (D_FF, D_MODEL), DEEPNORM_BETA * D_FF ** -0.5),
        "ln3_g": gain(ks[23], D_MODEL),
        "ln3_b": nrm(ks[24], (D_MODEL,), 0.01),
    }


def reference(x, ffn1_w_gate, ffn1_w_up, ffn1_w_down, ln1_g, ln1_b, w_in, conv_w, conv_b,
              dt_bias, a_log, d_skip, ssm_norm_w, w_ssm_out, w_att_out, b_gate, w_o,
              ln2_g, ln2_b, ffn2_w_gate, ffn2_w_up, ffn2_w_down, ln3_g, ln3_b):
    h = x
    for _ in range(DEPTH):
        h = layer_norm(DEEPNORM_ALPHA * h + 0.5 * swiglu(h, ffn1_w_gate, ffn1_w_up, ffn1_w_down), ln1_g, ln1_b)
        h = layer_norm(DEEPNORM_ALPHA * h + hybrid_mixer(h, w_in, conv_w, conv_b, dt_bias, a_log, d_skip,
                                                         ssm_norm_w, w_ssm_out, w_att_out, b_gate, w_o),
                       ln2_g, ln2_b)
        h = layer_norm(DEEPNORM_ALPHA * h + 0.5 * swiglu(h, ffn2_w_gate, ffn2_w_up, ffn2_w_down), ln3_g, ln3_b)
    return h
```

```python
import numpy as np
import concourse.bass as bass
import concourse.mybir as mybir
from concourse.bass_utils import run_bass_kernel_spmd

F32 = mybir.dt.float32
BF16 = mybir.dt.bfloat16
AF = mybir.ActivationFunctionType
ALU = mybir.AluOpType
AX = mybir.AxisListType

D = 1024
DFF = 2816
SEQ = 2048
BATCH = 32
NCORES = 8
TS = 512
NSUB = 4
DIN = 2048
NH = 32
NG = 8
ALPHA = 2.0 ** 0.25
LN_EPS = 1e-5
RMS_EPS = 1e-5
IN_COLS = 11296
OFF_Z, OFF_X, OFF_B, OFF_C, OFF_DT, OFF_Q, OFF_K, OFF_V, OFF_G = 0, 2048, 4096, 5120, 6144, 6176, 7200, 8224, 9248
NEG = -30000.0
ATT_SCALE = 1.0 / (128.0 ** 0.5)

BG = [0, 66]
BU = [22, 88]
BD = [44, 110]
BIN = 132
BDT = 220
BSO = 221
BAO = 237
BWO = 245
NBLK = 253

C_ID, C_U, C_SL, C_ONE, C_FUT, C_VAL, C_OWN, C_OH, C_NH = 0, 128, 256, 384, 512, 576, 640, 704, 1728
CST_COLS = 1792
CB_ID, CB_ONE, CB_OH = 0, 128, 256
CSTB_COLS = 1280
SC_CW, SC_CB, SC_NW, SC_BG, SC_DTB, SC_ALOG, SC_DSK = 0, 128, 160, 176, 192, 224, 256
SC_COLS = 288


class Tk:
    __slots__ = ("name", "w", "r", "al", "dsem", "dcnt", "excl", "last")

    def __init__(self, name, excl=False):
        self.name = name
        self.last = 0
        self.excl = excl
        self.w = None
        self.r = {}
        self.al = []
        self.dsem = None
        self.dcnt = 0


class Eng:
    def __init__(self, name):
        self.name = name
        self.ops = []
        self.sem = None
        self.count = 0
        self.known = {}


class Prog:
    def __init__(self, nc, dry):
        self.nc = nc
        self.dry = dry
        self.E = {n: Eng(n) for n in ("pe", "act", "dve", "pool", "sp")}
        self.nsem = 0
        self.ninst = 0
        self.dma_tks = []
        if not dry:
            self.new_epoch(0)

    def _newsem(self, name):
        s = self.nc.alloc_semaphore(name)
        self.nsem += 1
        return s

    def new_epoch(self, ep):
        if self.dry:
            return
        for n, e in self.E.items():
            e.sem = self._newsem("p_%s_%d" % (n, ep))
            e.count = 0

    def _deps(self, reads, writes, own=None):
        deps = {}

        def add(tok):
            if tok is None:
                return
            s, v = tok
            cur = deps.get(id(s))
            if cur is None or cur[1] < v:
                deps[id(s)] = (s, v)

        for t in reads:
            add(t.w)
            if t.excl:
                for tok in t.r.values():
                    if tok[0] is not own:
                        add(tok)
        for t in writes:
            add(t.w)
            for tok in t.r.values():
                add(tok)
            for a in t.al:
                add(a.w)
                for tok in a.r.values():
                    add(tok)
        return deps

    def _emit_waits(self, e, deps, excl_only=False):
        for s, v in deps.values():
            if s is e.sem and (e.name == "pe" or excl_only):
                continue
            if e.known.get(id(s), 0) >= v:
                continue
            e.known[id(s)] = v
            e.ops.append(lambda eng, s=s, v=v: eng.wait_ge(s, v))

    def op(self, en, fn, reads=(), writes=()):
        if self.dry:
            return
        e = self.E[en]
        self._emit_waits(e, self._deps(reads, writes, e.sem))
        e.count += 1
        sem, cnt = e.sem, e.count
        if isinstance(fn, (list, tuple)):
            fns = list(fn)
            for f in fns[:-1]:
                e.ops.append(f)
            last = fns[-1]
            e.ops.append(lambda eng, f=last, sem=sem: f(eng).then_inc(sem, 1))
            self.ninst += len(fns)
        else:
            e.ops.append(lambda eng, f=fn, sem=sem: f(eng).then_inc(sem, 1))
            self.ninst += 1
        tok = (sem, cnt)
        self.opn = getattr(self, "opn", 0) + 1
        for t in reads:
            t.r[id(sem)] = tok
            t.last = self.opn
        for t in writes:
            t.w = tok
            t.r = {}
            t.last = self.opn

    def dma(self, qn, out_ap, in_ap, reads=(), writes=(), semtk=None):
        if self.dry:
            return
        e = self.E[qn]
        self._emit_waits(e, self._deps(reads, writes, e.sem))
        tk = semtk if semtk is not None else (writes[0] if writes else reads[0])
        if tk.dsem is None:
            tk.dsem = self._newsem("d_" + tk.name)
            self.dma_tks.append(tk)
        sem = tk.dsem
        tk.dcnt += 1
        val = 16 * tk.dcnt
        e.ops.append(lambda eng, sem=sem, o=out_ap, i=in_ap: eng.dma_start(out=o, in_=i).then_inc(sem, 16))
        self.ninst += 1
        tok = (sem, val)
        for t in reads:
            t.r[id(sem)] = tok
        for t in writes:
            t.w = tok
            t.r = {}

    def final_wait(self, en, tks):
        if self.dry:
            return
        e = self.E[en]
        for o in self.E.values():
            if o.count > 0 and o is not e:
                e.ops.append(lambda eng, s=o.sem, v=o.count: eng.wait_ge(s, v))
        for t in self.dma_tks:
            e.ops.append(lambda eng, s=t.dsem, v=16 * t.dcnt: eng.wait_ge(s, v))
        for t in tks:
            if t.w is not None:
                s, v = t.w
                e.ops.append(lambda eng, s=s, v=v: eng.wait_ge(s, v))


def MM(out, lhsT, rhs, start=True, stop=True):
    return lambda e: e.matmul(out, lhsT=lhsT, rhs=rhs, start=start, stop=stop)


def TR(out, in_, ident):
    return lambda e: e.transpose(out, in_, ident)


def ACTF(out, in_, func, bias=None, scale=None, accum_out=None):
    kw = {}
    if bias is not None:
        kw["bias"] = bias
    if scale is not None:
        kw["scale"] = scale
    if accum_out is not None:
        kw["accum_out"] = accum_out
    return lambda e: e.activation(out=out, in_=in_, func=func, **kw)


def TT(out, in0, in1, op):
    return lambda e: e.tensor_tensor(out=out, in0=in0, in1=in1, op=op)


def TSC(out, in0, s1, s2, op0, op1=None):
    if op1 is None:
        return lambda e: e.tensor_scalar(out=out, in0=in0, scalar1=s1, scalar2=None, op0=op0)
    return lambda e: e.tensor_scalar(out=out, in0=in0, scalar1=s1, scalar2=s2, op0=op0, op1=op1)


def STT(out, in0, scalar, in1, op0, op1):
    return lambda e: e.scalar_tensor_tensor(out=out, in0=in0, scalar=scalar, in1=in1, op0=op0, op1=op1)


def CP(out, in_):
    return lambda e: e.tensor_copy(out=out, in_=in_)


class Mem:
    def __init__(self, nc):
        self.nc = nc
        self.base = ((nc.sbuf_base + 63) // 64) * 64
        self.top = nc.sbuf_top
        self.recs = []

    def at(self, name, shape, dt, off, tks):
        esz = 4 if dt == F32 else 2
        n = 1
        for s in shape[1:]:
            n *= s
        nbytes = n * esz
        assert off % 32 == 0, (name, off)
        assert self.base + off + nbytes <= self.top, (name, off, nbytes)
        t = self.nc.alloc_sbuf_tensor_at(name, list(shape), dt, offset=self.base + off)
        self.recs.append((off, off + nbytes, list(tks)))
        return t

    def finalize(self):
        n = len(self.recs)
        for i in range(n):
            a0, a1, ta = self.recs[i]
            for j in range(i + 1, n):
                b0, b1, tb = self.recs[j]
                if a0 < b1 and b0 < a1:
                    for x in ta:
                        for y in tb:
                            if x is not y:
                                x.al.append(y)
                                y.al.append(x)


def interleave(gens):
    gens = list(gens)
    while gens:
        for g_ in list(gens):
            try:
                next(g_)
            except StopIteration:
                gens.remove(g_)


class Buf:
    def __init__(self, t, tks):
        self.t = t
        self.tks = tks


class WStream:
    def __init__(self, P, slots, sched, wscr, wsc_tk, builder=None):
        self.builder = builder
        self.P = P
        self.slots = slots
        self.sched = sched
        self.pos = 0
        self.issued = 0
        self.wscr = wscr
        self.wsc_tk = wsc_tk

    def _issue(self, k):
        tens, tk = self.slots[k % len(self.slots)]
        b0, nb = self.sched[k]
        dst = tens[:, 0:nb * 1024].rearrange("p (c e) -> p c e", c=nb)
        src = self.wscr[b0:b0 + nb, :, :].rearrange("c p e -> p c e")
        ri = self.builder.region_of(b0)
        self.builder.ensure_region(ri)
        self.P.dma("sp", dst, src, reads=(self.builder.region_tk[ri],), writes=(tk,))

    def get(self, b0, nb):
        P = self.P
        k = self.pos
        self.pos += 1
        if P.dry:
            self.sched.append((b0, nb))
            return self.slots[k % len(self.slots)]
        assert self.sched[k] == (b0, nb), (k, self.sched[k], b0, nb)
        ahead = len(self.slots) - 2
        while self.issued < min(len(self.sched), k + ahead + 1):
            self._issue(self.issued)
            self.issued += 1
        self.builder.pump(2)
        return self.slots[k % len(self.slots)]


class Builder:
    def __init__(self, nseq, nseg_limit=None, dbg=None, stop=""):
        self.stop = stop
        self.nseq = nseq
        self.nseg_total = nseq * (SEQ // TS)
        if nseg_limit is not None:
            self.nseg_total = min(self.nseg_total, nseg_limit)
        self.dbg = dbg
        self.ntok = nseq * SEQ

    def declare(self):
        nc = bass.Bass("TRN2", target_bir_lowering=False)
        self.nc = nc
        ntok = self.ntok
        self.x_d = nc.dram_tensor("x", [ntok, D], F32, kind="ExternalInput").ap()
        self.wpack_d = nc.dram_tensor("wpack", [NBLK, 128, 1024], F32, kind="ExternalInput").ap()
        self.cst_d = nc.dram_tensor("cst_f32", [128, CST_COLS], F32, kind="ExternalInput").ap()
        self.small_d = nc.dram_tensor("smallc", [128, SC_COLS], F32, kind="ExternalInput").ap()
        self.lnp_d = nc.dram_tensor("lnp", [3, 128, 2 * D], F32, kind="ExternalInput").ap()
        self.out_d = nc.dram_tensor("out", [ntok, D], F32, kind="ExternalOutput").ap()
        self.dbg_d = None
        if self.dbg:
            self.dbg_d = nc.dram_tensor("dbg_out", [ntok, D], F32, kind="ExternalOutput").ap()
        self.dbgb_d = None
        if self.dbg:
            self.dbgb_d = nc.dram_tensor("dbgb_out", [self.nseg_total, 128, 8192], BF16, kind="ExternalOutput").ap()
        self.wscr = nc.dram_tensor("wscr", [NBLK, 128, 1024], BF16, kind="Internal").ap()
        self.kt_s = nc.dram_tensor("kt_s", [8, 128, SEQ], BF16, kind="Internal").ap()
        self.v_s = nc.dram_tensor("v_s", [SEQ, D], BF16, kind="Internal").ap()

        M = Mem(nc)
        self.M = M
        off = [0]

        def per(name, shape, dt, ntk=1):
            esz = 4 if dt == F32 else 2
            n = 1
            for s in shape[1:]:
                n *= s
            nb = ((n * esz + 63) // 64) * 64
            tks = [Tk("%s_%d" % (name, i)) for i in range(ntk)]
            t = M.at(name, shape, dt, off[0], tks)
            off[0] += nb
            return Buf(t, tks)

        self.cst = per("cst", [128, CST_COLS], F32)
        self.cstb = per("cstb", [128, CSTB_COLS], BF16)
        self.small = per("small", [128, 512], F32)
        self.lnp = per("lnpb", [128, 2 * D], F32)
        self.wdt = per("wdt", [128, 256], BF16)
        self.wsl = [per("wslot%d" % i, [128, 4096], BF16) for i in range(4)]
        self.res = per("res", [128, NSUB, D], F32, NSUB)
        self.ffnT = per("ffnT", [128, 8, TS], BF16, NSUB)
        self.h1T = per("h1T", [128, 8, TS], BF16, NSUB)
        self.stat = [per("stat%d" % i, [128, 16], F32) for i in range(4)]
        self.abc = per("abc", [128, 32], F32)
        self.carry = per("carry", [128, 32, 3], F32, NG)
        self.kmT = per("kmT", [128, 8, 8], F32)
        self.kmTb = per("kmTb", [128, 8, 8], BF16)
        self.st = per("st", [128, DIN], F32, NG)
        arena0 = off[0]
        self.arena0 = arena0

        def ar(name, shape, dt, o, ntk=1):
            tks = [Tk("%s_%d" % (name, i)) for i in range(ntk)]
            t = M.at(name, shape, dt, arena0 + o, tks)
            return Buf(t, tks)

        o = 0
        self.xst = ar("xst", [128, NSUB, D], F32, o, NSUB)
        self.ynT = ar("ynT", [128, 16, TS], BF16, o, 16); o += 16384
        self.OT = ar("OT", [128, 8, TS], BF16, o, 8); o += 8192
        self.QT = ar("QT", [128, 8, TS], BF16, o, 8); o_qt = o; o += 8192
        self.mgT = ar("mgT", [128, 8, TS], BF16, o_qt, 8)
        at0 = o
        self.kbuf = [ar("kbuf%d" % i, [128, SEQ], BF16, o + i * 4096) for i in range(2)]; o += 8192
        self.vbuf = [ar("vbuf%d" % i, [128, 16, 128], BF16, o + i * 4096) for i in range(2)]; o += 8192
        stg_end = o
        self.PT = [ar("PT%d" % i, [128, TS], BF16, o + i * 1024) for i in range(2)]; o += 2048
        self.rinv = ar("rinv", [128, TS], F32, o); o += 2048
        self.gm = ar("gm", [128, 8, 8], F32, o); o += 256
        self.top8 = ar("top8", [128, 8, 8], F32, o); o += 256
        self.sel = ar("sel", [128, 8, 8], F32, o); o += 256
        self.biasq = ar("biasq", [128, 8, 8], BF16, o); o += 128
        o += 128
        at1 = o
        self.biasT = ar("biasT", [128, 8, TS], BF16, o); o += 8192
        ssd0 = o
        self.raw = ar("raw", [128, 4, 516], F32, o, 4); o += 8256
        self.xc = ar("xc", [128, 4, TS], BF16, o, 4); o_xc = o; o += 4096
        self.sz = ar("sz", [128, 4, 256], F32, o, 4); o += 4096
        self.xc2 = ar("xc2", [128, 4, TS], BF16, o, 4); o += 4096
        self.sz2 = ar("sz2", [128, 4, 256], F32, o, 4); o += 4096
        self.KTseg = ar("KTseg", [128, 8, TS], BF16, ssd0)
        self.Vseg = ar("Vseg", [128, 4, D], BF16, o_xc)
        names = ["dtu", "dtv", "da", "acum", "eac", "edec", "wdf", "wex"]
        self.dtt = {}
        for nme in names:
            self.dtt[nme] = ar(nme, [128, 4, 32], F32, o); o += 512

        def ckset(i, o):
            dct = {}
            dct["xtok"] = ar("xtok%d" % i, [128, 384], BF16, o); o += 768
            dct["xdt"] = ar("xdt%d" % i, [128, 256], BF16, o); o += 512
            dct["xw"] = ar("xw%d" % i, [128, 256], BF16, o); o += 512
            dct["xd"] = ar("xd%d" % i, [128, 256], F32, o); o += 1024
            dct["CBm"] = ar("CBm%d" % i, [128, 128], F32, o); o += 512
            dct["R1"] = ar("R1%d" % i, [128, 512], F32, o)
            dct["L"] = ar("L%d" % i, [128, 512], F32, o); o += 2048
            dct["sq"] = dct["L"]
            dct["MT"] = ar("MT%d" % i, [128, 512], BF16, o); o += 1024
            dct["t1"] = ar("t1%d" % i, [128, 256], F32, o); o += 1024
            dct["yn"] = ar("yn%d" % i, [128, 256], BF16, o); o += 512
            dct["ss"] = ar("ss%d" % i, [128, 16], F32, o); o += 64
            return dct, o
        self.ck = []
        for i in range(2):
            dct, o = ckset(i, o)
            self.ck.append(dct)
        oa = at0
        for i in range(2, 4):
            dct, oa = ckset(i, oa)
            self.ck.append(dct)
        assert oa <= at1, (oa, at1)
        self.stbc = ar("stbc", [128, 4, 256], BF16, o); o += 2048
        ssd_end = o
        self.mt = [ar("mtmp%d" % i, [128, TS], F32, o + i * 2048) for i in range(4)]; o += 8192
        mix_end = o
        o = ssd0
        self.HT = ar("HT", [128, 22, TS], BF16, o, 22); o += 22528
        self.tmpA = [ar("tmpA%d" % i, [128, TS], F32, o + i * 2048) for i in range(2)]; o += 4096
        assert o <= ssd_end + 8192, (o, ssd_end)
        o = 0
        self.stgf = [ar("stgf%d" % i, [128, 4096], F32, o + i * 16384) for i in range(2)]; o += 32768
        self.stgb = [ar("stgb%d" % i, [128, 4096], BF16, o + i * 8192) for i in range(2)]; o += 16384
        assert o <= stg_end, (o, stg_end)
        assert M.base + arena0 + mix_end <= M.top, (M.base + arena0 + mix_end, M.top)
        M.finalize()
        self.banks = [nc.alloc_psum_tensor("bank%d" % i, [128, 512], F32) for i in range(8)]

    def build(self):
        self.declare()
        sched = []
        for dry in (True, False):
            self.P = Prog(self.nc, dry)
            self.bank_tk = [Tk("bank%d" % i, excl=True) for i in range(8)]
            self.bank_i = 0
            self.ring = list(range(8))
            self.held = set()
            self.dbgb_tk = Tk("dbgbd")
            self.wsc_tk = Tk("wscr")
            self.kts_tk = Tk("kts")
            self.vs_tk = Tk("vs")
            self.out_tk = Tk("outd")
            self.dbg_tk = Tk("dbgd")
            self.rot = {}
            self.W = WStream(self.P, [(b.t, b.tks[0]) for b in self.wsl], sched, self.wscr, self.wsc_tk, builder=self)
            self.wdt_loaded = False
            if not dry:
                self._reset_tks()
            self.program()
        self.emit()
        return self.nc

    def _reset_tks(self):
        for (_, _, tks) in self.M.recs:
            for t in tks:
                t.w = None
                t.r = {}
                t.dsem = None
                t.dcnt = 0

    def emit(self):
        nc, P = self.nc, self.P
        with nc.Block() as block:
            @block.tensor
            def _(e):
                for th in P.E["pe"].ops:
                    th(e)

            @block.scalar
            def _(e):
                for th in P.E["act"].ops:
                    th(e)

            @block.vector
            def _(e):
                for th in P.E["dve"].ops:
                    th(e)

            @block.gpsimd
            def _(e):
                for th in P.E["pool"].ops:
                    th(e)

            @block.sync
            def _(e):
                for th in P.E["sp"].ops:
                    th(e)

    def nb(self, hold=False):
        cand = [j for j in self.ring if j not in self.held]
        i = min(cand, key=lambda j: (self.bank_tk[j].last, j))
        self.bank_tk[i].last = getattr(self.P, "opn", 0) + 0.5
        if hold:
            self.held.add(i)
        return self.banks[i], self.bank_tk[i]

    def unhold(self, btk):
        self.held.discard(self.bank_tk.index(btk))

    def ev(self):
        return "act"

    def program(self):
        P = self.P
        cst, cstb = self.cst, self.cstb
        P.dma("sp", cst.t[:, :], self.cst_d[:, :], writes=cst.tks)
        P.dma("sp", self.small.t[:, 0:SC_COLS], self.small_d[:, :], writes=self.small.tks)
        P.op("dve", CP(cstb.t[:, CB_ID:CB_ID + 128], cst.t[:, C_ID:C_ID + 128]), reads=cst.tks, writes=cstb.tks)
        P.op("dve", CP(cstb.t[:, CB_ONE:CB_ONE + 128], cst.t[:, C_ONE:C_ONE + 128]), reads=cst.tks, writes=cstb.tks)
        P.op("dve", CP(cstb.t[:, CB_OH:CB_OH + 1024], cst.t[:, C_OH:C_OH + 1024]), reads=cst.tks, writes=cstb.tks)
        P.op("act", ACTF(self.abc.t[:, :], self.small.t[:, SC_ALOG:SC_ALOG + 32], AF.Exp), reads=self.small.tks, writes=self.abc.tks)
        P.op("dve", TSC(self.abc.t[:, :], self.abc.t[:, :], -1.0, None, ALU.mult), reads=self.abc.tks, writes=self.abc.tks)
        self.prologue()
        nps = SEQ // TS
        for seg in range(self.nseg_total):
            if seg > 0 and seg % 2 == 0:
                P.new_epoch(seg // 2)
            self.segment(seg, seg // nps, seg % nps)
        P.final_wait("sp", [self.out_tk, self.dbg_tk, self.dbgb_tk])

    REGIONS = [(0, 44), (44, 66), (132, 221), (221, 253), (66, 110), (110, 132)]

    def region_of(self, b):
        for i, (lo, hi) in enumerate(self.REGIONS):
            if lo <= b < hi:
                return i
        raise ValueError(b)

    def conv_init(self):
        self.conv_steps = []
        for ri, (lo, hi) in enumerate(self.REGIONS):
            for b0 in range(lo, hi, 4):
                self.conv_steps.append((ri, b0, min(4, hi - b0)))
        self.conv_pos = 0
        self.region_tk = [Tk("wreg%d" % i) for i in range(len(self.REGIONS))]

    def pump(self, n):
        P = self.P
        if P.dry:
            return
        while n > 0 and self.conv_pos < len(self.conv_steps):
            k = self.conv_pos
            ri, b0, nb = self.conv_steps[k]
            self.conv_pos += 1
            n -= 1
            sf, sbb = self.stgf[k % 2], self.stgb[k % 2]
            m = nb * 1024
            P.dma("pool", sf.t[:, 0:m].rearrange("p (c e) -> p c e", c=nb),
                  self.wpack_d[b0:b0 + nb, :, :].rearrange("c p e -> p c e"), writes=sf.tks)
            if k % 2 == 0:
                P.op("act", ACTF(sbb.t[:, 0:m], sf.t[:, 0:m], AF.Copy), reads=sf.tks, writes=sbb.tks)
            else:
                P.op("dve", CP(sbb.t[:, 0:m], sf.t[:, 0:m]), reads=sf.tks, writes=sbb.tks)
            P.dma("pool", self.wscr[b0:b0 + nb, :, :].rearrange("c p e -> p c e"),
                  sbb.t[:, 0:m].rearrange("p (c e) -> p c e", c=nb), reads=sbb.tks, writes=(self.region_tk[ri],))

    def ensure_region(self, ri):
        if self.P.dry:
            return
        while self.conv_pos < len(self.conv_steps) and self.conv_steps[self.conv_pos][0] <= ri:
            self.pump(1)

    def prologue(self):
        self.conv_init()
        self.ensure_region(1)

    def segment(self, seg, sq, t):
        P = self.P
        r0 = seg * TS
        res = self.res
        if seg == 0:
            self.load_x(0)
            self.x_front()
            self.x_res()
        self.ffn(0, self.ffnT, 0, self.h1T, None, r0, mid_hook=(self.x_res if seg > 0 else None))
        if self.dbg == "h1":
            return
        self.mixer(seg, sq, t, r0)
        if self.stop:
            return
        nxt = seg + 1 if seg + 1 < self.nseg_total else None
        self.ffn(1, self.ffnT, 2, None, self.out_d, r0, next_seg=nxt)

    def load_x(self, seg):
        r0 = seg * TS
        self.P.dma("sp", self.xst.t[:, :, :], self.x_d[r0:r0 + TS, :].rearrange("(s p) d -> p s d", p=128), writes=self.xst.tks)

    def x_front(self):
        for s in range(NSUB):
            self.transpose_sub(s, self.ffnT, src=self.xst)

    def x_res(self):
        for s in range(NSUB):
            self.P.op("act", ACTF(self.res.t[:, s, :], self.xst.t[:, s, :], AF.Copy, scale=ALPHA), reads=(self.xst.tks[s],), writes=(self.res.tks[s],))

    def load_ln(self, idx):
        self.P.dma("sp", self.lnp.t[:, :], self.lnp_d[idx, :, :], writes=self.lnp.tks)

    def scale_res(self, s):
        res = self.res
        self.P.op("act", ACTF(res.t[:, s, :], res.t[:, s, :], AF.Copy, scale=ALPHA), reads=(res.tks[s],), writes=(res.tks[s],))

    def transpose_sub(self, s, dstT, src=None):
        P = self.P
        res = src if src is not None else self.res
        ident = self.cst.t[:, C_ID:C_ID + 128]
        for half in range(2):
            bk, btk = self.nb()
            grp = [TR(bk[:, j * 128:(j + 1) * 128], res.t[:, s, (half * 4 + j) * 128:(half * 4 + j + 1) * 128], ident) for j in range(4)]
            P.op("pe", grp, reads=(res.tks[s],) + tuple(self.cst.tks), writes=(btk,))
            dst = dstT.t[:, half * 4:half * 4 + 4, s * 128:(s + 1) * 128]
            src = bk[:, :].rearrange("p (j q) -> p j q", j=4)
            en = self.ev()
            if en == "act":
                P.op("act", ACTF(dst, src, AF.Copy), reads=(btk,), writes=(dstT.tks[s],))
            else:
                P.op("dve", CP(dst, src), reads=(btk,), writes=(dstT.tks[s],))

    def ffn(self, fi, aT, lnidx, outT, out_d, r0, mid_hook=None, next_seg=None):
        P = self.P
        HT = self.HT
        for b in range(6):
            ncc = 4 if b < 5 else 2
            gs, gtk = self.W.get(BG[fi] + 4 * b, ncc)
            us, utk = self.W.get(BU[fi] + 4 * b, ncc)
            for cc in range(ncc):
                fj = 4 * b + cc
                bg, bgtk = self.nb()
                bu, butk = self.nb()
                P.op("pe", [MM(bg[:, :], gs[:, cc * 1024 + kc * 128:cc * 1024 + (kc + 1) * 128], aT.t[:, kc, :], kc == 0, kc == 7) for kc in range(8)],
                     reads=(gtk,) + tuple(aT.tks), writes=(bgtk,))
                P.op("pe", [MM(bu[:, :], us[:, cc * 1024 + kc * 128:cc * 1024 + (kc + 1) * 128], aT.t[:, kc, :], kc == 0, kc == 7) for kc in range(8)],
                     reads=(utk,) + tuple(aT.tks), writes=(butk,))
                tA = self.tmpA[fj % 2]
                P.op("act", ACTF(tA.t[:, :], bg[:, :], AF.Silu), reads=(bgtk,), writes=tA.tks)
                P.op("dve", TT(HT.t[:, fj, :], tA.t[:, :], bu[:, :], ALU.mult), reads=tuple(tA.tks) + (butk,), writes=(HT.tks[fj],))
        self.load_ln(lnidx)
        if mid_hook is not None:
            mid_hook()
        if next_seg is not None:
            self.load_x(next_seg)
        for ps in range(2):
            bks = [[self.nb() for h in range(2)] for s2 in range(2)]
            for blk in range(6):
                nfc = 4 if blk < 5 else 2
                ds, dtk = self.W.get(BD[fi] + 4 * blk, nfc)
                grp = []
                for fcl in range(nfc):
                    fc = 4 * blk + fcl
                    for s2 in range(2):
                        s = 2 * ps + s2
                        for h in range(2):
                            grp.append(MM(bks[s2][h][0][:, :], HT.t[:, fc, s * 128:(s + 1) * 128],
                                          ds[:, fcl * 1024 + h * 512:fcl * 1024 + (h + 1) * 512], fc == 0, fc == 21))
                P.op("pe", grp, reads=(dtk,) + tuple(HT.tks[4 * blk:4 * blk + nfc]),
                     writes=tuple(bks[s2][h][1] for s2 in range(2) for h in range(2)))
            tag = "h1" if fi == 0 else "out"
            for s2 in range(2):
                self.ln_math(2 * ps + s2, bks[s2], 0.5, out_d, r0, tag)
            if ps == 0 and next_seg is not None:
                self.x_front()
            if ps == 1:
                for s in range(4):
                    self.ln_post(s, outT)

    def ln_math(self, s, bk2, scale, out_d, r0, tag):
        P = self.P
        res, lnp = self.res, self.lnp
        rt = res.tks[s]
        st = self.stat[s]
        for h in range(2):
            P.op("dve", STT(res.t[:, s, h * 512:(h + 1) * 512], bk2[h][0][:, :], scale, res.t[:, s, h * 512:(h + 1) * 512], ALU.mult, ALU.add),
                 reads=(bk2[h][1], rt), writes=(rt,))
        for h in range(2):
            P.op("dve", lambda e, h=h: e.bn_stats(out=st.t[:, 6 * h:6 * h + 6], in_=res.t[:, s, h * 512:(h + 1) * 512]), reads=(rt,), writes=st.tks)
        P.op("dve", lambda e: e.bn_aggr(out=st.t[:, 12:14], in_=st.t[:, 0:12]), reads=st.tks, writes=st.tks)
        P.op("pool", TSC(st.t[:, 14:15], st.t[:, 13:14], LN_EPS, None, ALU.add), reads=st.tks, writes=st.tks)
        P.op("pool", TT(st.t[:, 15:16], st.t[:, 14:15], self.cst.t[:, C_NH:C_NH + 1], ALU.pow), reads=tuple(st.tks) + tuple(self.cst.tks), writes=st.tks)
        P.op("dve", TSC(res.t[:, s, :], res.t[:, s, :], st.t[:, 12:13], st.t[:, 15:16], ALU.subtract, ALU.mult), reads=(rt,) + tuple(st.tks), writes=(rt,))
        P.op("dve", TT(res.t[:, s, :], res.t[:, s, :], lnp.t[:, 0:D], ALU.mult), reads=(rt,) + tuple(lnp.tks), writes=(rt,))
        P.op("pool", TT(res.t[:, s, :], res.t[:, s, :], lnp.t[:, D:2 * D], ALU.add), reads=(rt,) + tuple(lnp.tks), writes=(rt,))
        if self.dbg == tag and tag != "out":
            P.dma("sp", self.dbg_d[r0 + s * 128:r0 + (s + 1) * 128, :], res.t[:, s, :], reads=(rt,), writes=(self.dbg_tk,))
        if out_d is not None:
            P.dma("sp", out_d[r0 + s * 128:r0 + (s + 1) * 128, :], res.t[:, s, :], reads=(rt,), writes=(self.out_tk,))

    def ln_post(self, s, outT):
        if outT is not None:
            self.transpose_sub(s, outT)
            self.scale_res(s)

    def dump_bf(self, seg, ap, n, tks):
        self.P.dma("sp", self.dbgb_d[seg, :, 0:n], ap, reads=tuple(tks), writes=(self.dbgb_tk,))

    def evac(self, dst, src, btk, wtks, en=None):
        en = en or self.ev()
        if en == "act":
            self.P.op("act", ACTF(dst, src, AF.Copy), reads=(btk,), writes=tuple(wtks))
        else:
            self.P.op(en, CP(dst, src), reads=(btk,), writes=tuple(wtks))

    def mixer(self, seg, sq, t, r0):
        P = self.P
        self.ring = list(range(8))
        if t == 0:
            P.op("pool", lambda e: e.memset(self.st.t[:, :], 0.0), writes=self.st.tks)
            P.op("pool", lambda e: e.memset(self.carry.t[:, :, :], 0.0), writes=self.carry.tks)
            P.op("pool", lambda e: e.memset(self.kmT.t[:, :, :], 0.0), writes=self.kmT.tks)
            P.op("pool", lambda e: e.memset(self.kmTb.t[:, :, :], 0.0), writes=self.kmTb.tks)
        stop = self.stop
        self.qkv(seg, t)
        if stop == "qkv":
            return
        self.gate_bias(t)
        if stop == "gate":
            return
        self.ssd(seg, t)
        if self.dbg == "yn":
            self.dump_bf(seg, self.ynT.t[:, :, :].rearrange("p a b -> p (a b)"), 8192, self.ynT.tks)
        if stop == "ssd":
            return
        self.attention(seg, t)
        if self.dbg == "ya":
            self.dump_bf(seg, self.OT.t[:, :, :].rearrange("p a b -> p (a b)"), 4096, self.OT.tks)
        if stop == "att":
            return
        self.merge(seg, t, r0)

    def qkv(self, seg, t):
        P = self.P
        h1T, QT, KTseg, Vseg = self.h1T, self.QT, self.KTseg, self.Vseg
        for hq in range(2):
            qs, qtk = self.W.get(BIN + 4 * hq, 4)
            for cc in range(4):
                h = 4 * hq + cc
                bk, btk = self.nb()
                P.op("pe", [MM(bk[:, :], qs[:, cc * 1024 + kc * 128:cc * 1024 + (kc + 1) * 128], h1T.t[:, kc, :], kc == 0, kc == 7) for kc in range(8)],
                     reads=(qtk,) + tuple(h1T.tks), writes=(btk,))
                self.evac(QT.t[:, h, :], bk[:, :], btk, (QT.tks[h],))
        for hk in range(2):
            ks, ktk = self.W.get(BIN + 8 + 4 * hk, 4)
            for cc in range(4):
                h = 4 * hk + cc
                bk, btk = self.nb()
                P.op("pe", [MM(bk[:, :], ks[:, cc * 1024 + kc * 128:cc * 1024 + (kc + 1) * 128], h1T.t[:, kc, :], kc == 0, kc == 7) for kc in range(8)],
                     reads=(ktk,) + tuple(h1T.tks), writes=(btk,))
                for b in range(2):
                    P.op("act", ACTF(KTseg.t[:, h, b * 256:(b + 1) * 256], bk[:, b * 256:(b + 1) * 256], AF.Identity,
                                     accum_out=self.kmT.t[:, h, 2 * t + b:2 * t + b + 1]), reads=(btk,), writes=tuple(KTseg.tks) + tuple(self.kmT.tks))
        for vb in range(2):
            vs, vtk = self.W.get(BIN + 16 + 4 * vb, 4)
            for s in range(NSUB):
                bk, btk = self.nb()
                P.op("pe", [MM(bk[:, :], h1T.t[:, kc, s * 128:(s + 1) * 128], vs[:, kc * 512:(kc + 1) * 512], kc == 0, kc == 7) for kc in range(8)],
                     reads=(vtk, h1T.tks[s]), writes=(btk,))
                self.evac(Vseg.t[:, s, vb * 512:(vb + 1) * 512], bk[:, :], btk, Vseg.tks)
        if True:
            P.dma("sp", self.kt_s[:, :, t * TS:(t + 1) * TS].rearrange("h p q -> p h q"), KTseg.t[:, :, :], reads=KTseg.tks, writes=(self.kts_tk,))
        if True:
            P.dma("sp", self.v_s[t * TS:(t + 1) * TS, :].rearrange("(s p) d -> p s d", p=128), Vseg.t[:, :, :], reads=Vseg.tks, writes=(self.vs_tk,))
        P.op("dve", CP(self.kmTb.t[:, :, 2 * t:2 * t + 2], self.kmT.t[:, :, 2 * t:2 * t + 2]), reads=self.kmT.tks, writes=self.kmTb.tks)

    def gate_bias(self, t):
        P = self.P
        cst, cstb = self.cst.t, self.cstb.t
        QT, gm, top8, sel, biasq, biasT = self.QT, self.gm, self.top8, self.sel, self.biasq, self.biasT
        ident_b = cstb[:, CB_ID:CB_ID + 128]
        for s in range(NSUB):
            qb = 2 * t + s // 2
            bk, btk = self.nb()
            P.op("pe", [MM(bk[:, h * 8:(h + 1) * 8], QT.t[:, h, s * 128:(s + 1) * 128], self.kmTb.t[:, h, :]) for h in range(8)],
                 reads=tuple(QT.tks) + tuple(self.kmTb.tks), writes=(btk,))
            bc = lambda c0: cst[:, c0 + qb * 8:c0 + qb * 8 + 8].unsqueeze(1).to_broadcast([128, 8, 8])
            P.op("dve", TT(gm.t[:, :, :], bk[:, 0:64].rearrange("p (h j) -> p h j", h=8), bc(C_FUT), ALU.add), reads=(btk,) + tuple(self.cst.tks), writes=gm.tks)
            for h in range(8):
                P.op("dve", lambda e, h=h: e.max(out=top8.t[:, h, :], in_=gm.t[:, h, :]), reads=gm.tks, writes=top8.tks)
            P.op("dve", TT(sel.t[:, :, :], gm.t[:, :, :], top8.t[:, :, 2:3].to_broadcast([128, 8, 8]), ALU.is_ge), reads=tuple(gm.tks) + tuple(top8.tks), writes=sel.tks)
            P.op("dve", TT(sel.t[:, :, :], sel.t[:, :, :], bc(C_VAL), ALU.mult), reads=sel.tks, writes=sel.tks)
            P.op("dve", TT(sel.t[:, :, :], sel.t[:, :, :], bc(C_OWN), ALU.add), reads=sel.tks, writes=sel.tks)
            P.op("dve", TSC(biasq.t[:, :, :], sel.t[:, :, :], -NEG, NEG, ALU.mult, ALU.add), reads=sel.tks, writes=biasq.tks)
            bk2, btk2 = self.nb()
            b2 = bk2[:, :].bitcast(BF16)
            P.op("pe", [TR(b2[0:8, h * 128:(h + 1) * 128], biasq.t[:, h, :], ident_b) for h in range(8)], reads=tuple(biasq.tks) + tuple(self.cstb.tks), writes=(btk2,))
            P.op("act", ACTF(biasT.t[0:8, :, s * 128:(s + 1) * 128], b2[0:8, 0:1024].rearrange("p (h q) -> p h q", h=8), AF.Copy), reads=(btk2,), writes=biasT.tks)

    def ckpt(self, n):
        return False

    def ssd(self, seg, t):
        P = self.P
        cst, cstb, small = self.cst.t, self.cstb.t, self.small.t
        h1T, dtt = self.h1T, self.dtt
        ident_b = cstb[:, CB_ID:CB_ID + 128]
        U = cst[:, C_U:C_U + 128]
        ctk = tuple(self.cst.tks)
        v3 = lambda ap: ap.rearrange("p (s h) -> p s h", s=4)
        _ndt = 99
        _c = [0]
        def _ok():
            _c[0] += 1
            return _c[0] <= _ndt
        if not self.wdt_loaded and not P.dry:
            self.ensure_region(2)
            P.dma("sp", self.wdt.t[:, :], self.wscr[BDT, :, 0:256], reads=(self.region_tk[2],), writes=self.wdt.tks)
            self.wdt_loaded = True
        bk, btk = self.nb()
        if _ok():
            P.op("pe", [MM(bk[:, s * 32:(s + 1) * 32], h1T.t[:, kc, s * 128:(s + 1) * 128], self.wdt.t[:, kc * 32:(kc + 1) * 32], kc == 0, kc == 7)
                        for s in range(4) for kc in range(8)], reads=tuple(h1T.tks) + tuple(self.wdt.tks), writes=(btk,))
        dtu, dtv, da, acum, eac, edec, wdf, wex = [dtt[n] for n in ("dtu", "dtv", "da", "acum", "eac", "edec", "wdf", "wex")]
        if _ok():
            P.op("dve", TT(dtu.t[:, :, :], v3(bk[:, 0:128]), small[:, SC_DTB:SC_DTB + 32].unsqueeze(1).to_broadcast([128, 4, 32]), ALU.add),
                 reads=(btk,) + tuple(self.small.tks), writes=dtu.tks)
        if _ok():
            P.op("act", ACTF(dtv.t[:, :, :], dtu.t[:, :, :], AF.Exp), reads=dtu.tks, writes=dtv.tks)
        if _ok():
            P.op("act", ACTF(dtv.t[:, :, :], dtv.t[:, :, :], AF.Ln, bias=1.0), reads=dtv.tks, writes=dtv.tks)
        if _ok():
            P.op("dve", TT(da.t[:, :, :], dtv.t[:, :, :], self.abc.t[:, :].unsqueeze(1).to_broadcast([128, 4, 32]), ALU.mult),
                 reads=tuple(dtv.tks) + tuple(self.abc.tks), writes=da.tks)
        bA, bAtk = self.nb()
        if _ok():
            P.op("pe", [MM(bA[:, s * 32:(s + 1) * 32], U, da.t[:, s, :]) for s in range(4)], reads=ctk + tuple(da.tks), writes=(bAtk,))
        if _ok():
            P.op("act", ACTF(eac.t[:, :, :], v3(bA[:, 0:128]), AF.Exp), reads=(bAtk,), writes=eac.tks)
        if _ok():
            P.op("dve", CP(acum.t[:, :, :], v3(bA[:, 0:128])), reads=(bAtk,), writes=acum.tks)
        bL, bLtk = self.nb()
        if _ok():
            P.op("pe", [MM(bL[:, s * 32:(s + 1) * 32], cst[:, C_ONE:C_ONE + 128], da.t[:, s, :]) for s in range(4)], reads=ctk + tuple(da.tks), writes=(bLtk,))
        if _ok():
            P.op("act", ACTF(edec.t[:, :, :], v3(bL[:, 0:128]), AF.Exp), reads=(bLtk,), writes=edec.tks)
        if _ok():
            P.op("dve", TT(wdf.t[:, :, :], v3(bL[:, 0:128]), acum.t[:, :, :], ALU.subtract), reads=(bLtk,) + tuple(acum.tks), writes=wdf.tks)
        if _ok():
            P.op("act", ACTF(wex.t[:, :, :], wdf.t[:, :, :], AF.Exp), reads=wdf.tks, writes=wex.tks)
        b4 = lambda buf, c, g, n: buf.t[:, c, 4 * g:4 * g + 4].unsqueeze(2).to_broadcast([128, 4, n])
        if self.ckpt(1):
            return
        raw, xc, sz, st, carry, ynT = self.raw, self.xc, self.sz, self.st, self.carry, self.ynT
        xcs, szs = [self.xc, self.xc2], [self.sz, self.sz2]

        def phaseA(g):
            xc, sz = xcs[g % 2], szs[g % 2]
            zx, zxtk = self.W.get(BIN + 24 + 6 * g, 4)
            bc_, bctk = self.W.get(BIN + 24 + 6 * g + 4, 2)
            P.op("pool", CP(raw.t[:, :, 0:3], carry.t[:, 4 * g:4 * g + 4, :]), reads=(carry.tks[g],), writes=raw.tks)
            srcs = [(zx, zxtk, 2048, 2 * g), (zx, zxtk, 3072, 2 * g + 1), (bc_, bctk, 0, 16 + g), (bc_, bctk, 1024, 24 + g)]

            def conv_chain(i, sl, sltk, o0, cch):
                bk, btk = self.nb()
                P.op("pe", [MM(bk[:, :], sl[:, o0 + kc * 128:o0 + (kc + 1) * 128], h1T.t[:, kc, :], kc == 0, kc == 7) for kc in range(8)],
                     reads=(sltk,) + tuple(h1T.tks), writes=(btk,))
                yield
                P.op("act", ACTF(raw.t[:, i, 3:515], bk[:, :], AF.Copy), reads=(btk,), writes=(raw.tks[i],))
                yield
                acc = self.mt[i]
                cw = lambda k: small[:, SC_CW + cch * 4 + k:SC_CW + cch * 4 + k + 1]
                P.op("act", ACTF(acc.t[:, :], raw.t[:, i, 0:512], AF.Identity, bias=small[:, SC_CB + cch:SC_CB + cch + 1], scale=cw(0)),
                     reads=(raw.tks[i],) + tuple(self.small.tks), writes=acc.tks)
                yield
                for k in range(1, 4):
                    P.op("dve", STT(acc.t[:, :], raw.t[:, i, k:k + 512], cw(k), acc.t[:, :], ALU.mult, ALU.add), reads=(raw.tks[i],) + tuple(acc.tks), writes=acc.tks)
                    yield
                P.op("act", ACTF(xc.t[:, i, :], acc.t[:, :], AF.Silu), reads=acc.tks, writes=(xc.tks[i],))
                yield

            def z_chain(pair):
                bz, bztk = self.nb()
                for s_ in (2 * pair, 2 * pair + 1):
                    P.op("pe", [MM(bz[:, (s_ % 2) * 256:(s_ % 2 + 1) * 256], h1T.t[:, kc, s_ * 128:(s_ + 1) * 128], zx[:, kc * 256:(kc + 1) * 256], kc == 0, kc == 7) for kc in range(8)],
                         reads=(zxtk, h1T.tks[s_]), writes=(bztk,))
                yield
                yield
                yield
                yield
                yield
                yield
                for s_ in (2 * pair, 2 * pair + 1):
                    P.op("act", ACTF(sz.t[:, s_, :], bz[:, (s_ % 2) * 256:(s_ % 2 + 1) * 256], AF.Silu), reads=(bztk,), writes=(sz.tks[s_],))
                yield

            def tail():
                for _ in range(8):
                    yield
                P.op("pool", CP(carry.t[:, 4 * g:4 * g + 4, :], raw.t[:, :, 512:515]), reads=raw.tks, writes=(carry.tks[g],))
                yield
            return [conv_chain(i, *srcs[i]) for i in range(4)] + [z_chain(0), z_chain(1), tail()]

        interleave(phaseA(0))
        for g in range(NG):
            xc, sz = xcs[g % 2], szs[g % 2]
            v4 = lambda ap: ap.rearrange("p (j q) -> p j q", j=4)
            bIs = [None] * 4
            bYs = [None] * 4

            def phaseB(c):
                k = self.ck[c]
                cs = slice(c * 128, (c + 1) * 128)
                bT, bTtk = self.nb()
                bTb = bT[:, :].bitcast(BF16)
                P.op("pe", [TR(bTb[:, 0:128], xc.t[:, 0, cs], ident_b), TR(bTb[:, 128:256], xc.t[:, 1, cs], ident_b), TR(bTb[:, 256:384], xc.t[:, 2, cs], ident_b)],
                     reads=tuple(xc.tks[0:3]) + tuple(self.cstb.tks), writes=(bTtk,))
                yield
                P.op("act", ACTF(k["xtok"].t[:, 0:384], bTb[:, 0:384], AF.Copy), reads=(bTtk,), writes=k["xtok"].tks)
                P.op("dve", TT(v4(k["xdt"].t[:, :]), v4(bTb[:, 0:256]), b4(dtv, c, g, 64), ALU.mult), reads=(bTtk,) + tuple(dtv.tks), writes=k["xdt"].tks)
                P.op("pool", TT(v4(k["R1"].t[:, :]), U.unsqueeze(1).to_broadcast([128, 4, 128]), b4(da, c, g, 128), ALU.mult), reads=ctk + tuple(da.tks), writes=k["R1"].tks)
                yield
                bCB, bCBtk = self.nb()
                P.op("pe", [MM(bCB[:, 0:128], xc.t[:, 2, cs], xc.t[:, 3, cs])], reads=(xc.tks[2], xc.tks[3]), writes=(bCBtk,))
                bD, bDtk = self.nb()
                P.op("pe", [MM(bD[:, :], cst[:, C_SL:C_SL + 128], k["R1"].t[:, :])], reads=ctk + tuple(k["R1"].tks), writes=(bDtk,))
                yield
                P.op("pool", TT(v4(k["xw"].t[:, :]), v4(k["xdt"].t[:, :]), b4(wex, c, g, 64), ALU.mult), reads=tuple(k["xdt"].tks) + tuple(wex.tks), writes=k["xw"].tks)
                P.op("dve", TT(k["CBm"].t[:, :], bCB[:, 0:128], U, ALU.mult), reads=(bCBtk,) + ctk, writes=k["CBm"].tks)
                P.op("act", ACTF(k["L"].t[:, :], bD[:, :], AF.Exp), reads=(bDtk,), writes=k["L"].tks)
                yield
                bI, bItk = self.nb(hold=True)
                P.op("pe", [MM(bI[:, 0:256], k["xtok"].t[:, 256:384], k["xw"].t[:, :])], reads=tuple(k["xtok"].tks) + tuple(k["xw"].tks), writes=(bItk,))
                bIs[c] = (bI, bItk)
                P.op("dve", TT(v4(k["MT"].t[:, :]), v4(k["L"].t[:, :]), k["CBm"].t[:, :].unsqueeze(1).to_broadcast([128, 4, 128]), ALU.mult),
                     reads=tuple(k["L"].tks) + tuple(k["CBm"].tks), writes=k["MT"].tks)
                P.op("pool", TT(v4(k["xd"].t[:, :]), v4(k["xtok"].t[:, 0:256]), small[:, SC_DSK + 4 * g:SC_DSK + 4 * g + 4].unsqueeze(2).to_broadcast([128, 4, 64]), ALU.mult),
                     reads=tuple(k["xtok"].tks) + tuple(self.small.tks), writes=k["xd"].tks)
                yield
                bY, bYtk = self.nb(hold=True)
                bYs[c] = (bY, bYtk)
                P.op("pe", [MM(bY[:, j * 64:(j + 1) * 64], k["MT"].t[:, j * 128:(j + 1) * 128], k["xdt"].t[:, j * 64:(j + 1) * 64]) for j in range(4)],
                     reads=tuple(k["MT"].tks) + tuple(k["xdt"].tks), writes=(bYtk,))
                yield

            def phaseD(c):
                k = self.ck[c]
                cs = slice(c * 128, (c + 1) * 128)
                bY, bYtk = bYs[c]
                P.op("pe", [MM(bY[:, 256:512], xc.t[:, 3, cs], self.stbc.t[:, c, :])], reads=(xc.tks[3],) + tuple(self.stbc.tks), writes=(bYtk,))
                yield
                t1 = k["t1"]
                P.op("dve", TT(v4(t1.t[:, :]), v4(bY[:, 256:512]), b4(eac, c, g, 64), ALU.mult), reads=(bYtk,) + tuple(eac.tks), writes=t1.tks)
                P.op("dve", TT(t1.t[:, :], t1.t[:, :], bY[:, 0:256], ALU.add), reads=tuple(t1.tks) + (bYtk,), writes=t1.tks)
                self.unhold(bYtk)
                yield
                P.op("pool", TT(t1.t[:, :], t1.t[:, :], k["xd"].t[:, :], ALU.add), reads=tuple(t1.tks) + tuple(k["xd"].tks), writes=t1.tks)
                P.op("pool", TT(t1.t[:, :], t1.t[:, :], sz.t[:, c, :], ALU.mult), reads=tuple(t1.tks) + (sz.tks[c],), writes=t1.tks)
                yield
                ss = k["ss"]
                P.op("act", ACTF(k["sq"].t[:, 0:256], t1.t[:, :], AF.Square, accum_out=ss.t[:, 0:1]), reads=t1.tks, writes=tuple(k["sq"].tks) + tuple(ss.tks))
                yield
                P.op("act", ACTF(ss.t[:, 1:2], ss.t[:, 0:1], AF.Ln, bias=cst[:, C_NH + 1:C_NH + 2], scale=1.0 / 256.0), reads=tuple(ss.tks) + ctk, writes=ss.tks)
                P.op("act", ACTF(ss.t[:, 2:3], ss.t[:, 1:2], AF.Exp, scale=-0.5), reads=ss.tks, writes=ss.tks)
                yield
                P.op("dve", TSC(k["yn"].t[:, :], t1.t[:, :], ss.t[:, 2:3], None, ALU.mult), reads=tuple(t1.tks) + tuple(ss.tks), writes=k["yn"].tks)
                yield

            def phaseD2(c):
                k = self.ck[c]
                cs = slice(c * 128, (c + 1) * 128)
                bT2, bT2tk = self.nb()
                b2 = bT2[:, :].bitcast(BF16)
                P.op("pe", [TR(b2[:, 0:128], k["yn"].t[:, 0:128], ident_b), TR(b2[:, 128:256], k["yn"].t[:, 128:256], ident_b)],
                     reads=tuple(k["yn"].tks) + tuple(self.cstb.tks), writes=(bT2tk,))
                yield
                for i2 in range(2):
                    P.op("act", ACTF(ynT.t[:, 2 * g + i2, cs], b2[:, i2 * 128:(i2 + 1) * 128], AF.Identity, scale=small[:, SC_NW + 2 * g + i2:SC_NW + 2 * g + i2 + 1]),
                         reads=(bT2tk,) + tuple(self.small.tks), writes=(ynT.tks[2 * g + i2],))
                yield

            interleave([phaseB(c) for c in range(NSUB)])
            stg = st.t[:, g * 256:(g + 1) * 256]
            for c in range(NSUB):
                P.op("dve", CP(self.stbc.t[:, c, :], stg), reads=(st.tks[g],), writes=self.stbc.tks)
                P.op("dve", TT(v4(stg), v4(stg), b4(edec, c, g, 64), ALU.mult), reads=(st.tks[g],) + tuple(edec.tks), writes=(st.tks[g],))
                P.op("dve", TT(stg, stg, bIs[c][0][:, 0:256], ALU.add), reads=(st.tks[g], bIs[c][1]), writes=(st.tks[g],))
                self.unhold(bIs[c][1])
            interleave([phaseD(c) for c in range(NSUB)])
            if g + 1 < NG:
                interleave(phaseA(g + 1))
            interleave([phaseD2(c) for c in range(NSUB)])

    def attention(self, seg, t):
        P = self.P
        cst, cstb = self.cst.t, self.cstb.t
        QT, OT, biasT = self.QT, self.OT, self.biasT
        nkt = 4 * (t + 1)
        ones_b = cstb[:, CB_ONE:CB_ONE + 128]
        self.ring = [0, 1, 2, 3]
        def load_kv(h):
            kb, vb = self.kbuf[h % 2], self.vbuf[h % 2]
            P.dma("sp", kb.t[:, 0:nkt * 128], self.kt_s[h, :, 0:nkt * 128], reads=(self.kts_tk,), writes=kb.tks)
            P.dma("sp", vb.t[:, 0:nkt, :], self.v_s[0:nkt * 128, h * 128:(h + 1) * 128].rearrange("(k p) d -> p k d", p=128), reads=(self.vs_tk,), writes=vb.tks)

        load_kv(0)
        for h in range(8):
            kb, vb = self.kbuf[h % 2], self.vbuf[h % 2]
            if h + 1 < 8:
                load_kv(h + 1)
            io = 4 + 2 * (h % 2)
            bO, bOtk, bR, bRtk = self.banks[io], self.bank_tk[io], self.banks[io + 1], self.bank_tk[io + 1]

            def qk(kt):
                qlo = 0 if kt < 4 * t else (kt - 4 * t) * 128
                bS, bStk = self.nb()
                PT = self.PT[kt % 2]
                P.op("pe", [MM(bS[:, qlo:512], kb.t[:, kt * 128:(kt + 1) * 128], QT.t[:, h, qlo:512], True, False),
                            MM(bS[:, qlo:512], cstb[0:8, CB_OH + (kt // 2) * 128:CB_OH + (kt // 2 + 1) * 128], biasT.t[0:8, h, qlo:512], False, True)],
                     reads=tuple(kb.tks) + (QT.tks[h],) + tuple(biasT.tks) + tuple(self.cstb.tks), writes=(bStk,))
                P.op("act", ACTF(PT.t[:, qlo:512], bS[:, qlo:512], AF.Exp, scale=ATT_SCALE), reads=(bStk,), writes=PT.tks)
                if kt >= 4 * t:
                    P.op("pool", TT(PT.t[:, qlo:qlo + 128], PT.t[:, qlo:qlo + 128], cst[:, C_U:C_U + 128], ALU.mult), reads=tuple(PT.tks) + tuple(self.cst.tks), writes=PT.tks)
                return PT, qlo

            cur = qk(0)
            for kt in range(nkt):
                nxt = qk(kt + 1) if kt + 1 < nkt else None
                PT, qlo = cur
                P.op("pe", [MM(bO[:, qlo:512], vb.t[:, kt, :], PT.t[:, qlo:512], kt == 0, kt == nkt - 1),
                            MM(bR[:, qlo:512], ones_b, PT.t[:, qlo:512], kt == 0, kt == nkt - 1)],
                     reads=tuple(vb.tks) + tuple(PT.tks) + tuple(self.cstb.tks), writes=(bOtk, bRtk))
                cur = nxt
            P.op("dve", lambda e, bR=bR: e.reciprocal(out=self.rinv.t[:, :], in_=bR[:, :]), reads=(bRtk,), writes=self.rinv.tks)
            P.op("dve", TT(OT.t[:, h, :], bO[:, :], self.rinv.t[:, :], ALU.mult), reads=(bOtk,) + tuple(self.rinv.tks), writes=(OT.tks[h],))
        self.ring = list(range(8))

    def merge(self, seg, t, r0):
        P = self.P
        self.load_ln(1)
        small = self.small.t
        h1T, ynT, OT, mgT, mt = self.h1T, self.ynT, self.OT, self.mgT, self.mt
        for j in range(8):
            so, sotk = self.W.get(BSO + 2 * j, 2)
            b1, b1tk = self.nb()
            P.op("pe", [MM(b1[:, :], so[:, kc * 128:(kc + 1) * 128], ynT.t[:, kc, :], kc == 0, kc == 15) for kc in range(16)], reads=(sotk,) + tuple(ynT.tks), writes=(b1tk,))
            ao, aotk = self.W.get(BAO + j, 1)
            b2, b2tk = self.nb()
            P.op("pe", [MM(b2[:, :], ao[:, kc * 128:(kc + 1) * 128], OT.t[:, kc, :], kc == 0, kc == 7) for kc in range(8)], reads=(aotk,) + tuple(OT.tks), writes=(b2tk,))
            gt, gttk = self.W.get(BIN + 72 + 2 * j, 2)
            b3, b3tk = self.nb()
            P.op("pe", [MM(b3[:, :], gt[:, kc * 128:(kc + 1) * 128], h1T.t[:, kc, :], kc == 0, kc == 7) for kc in range(8)], reads=(gttk,) + tuple(h1T.tks), writes=(b3tk,))
            b4_, b4tk = self.nb()
            P.op("pe", [MM(b4_[:, :], gt[:, 1024 + kc * 128:1024 + (kc + 1) * 128], h1T.t[:, kc, :], kc == 0, kc == 7) for kc in range(8)], reads=(gttk,) + tuple(h1T.tks), writes=(b4tk,))
            P.op("act", ACTF(mt[0].t[:, :], b3[:, :], AF.Sigmoid, bias=small[:, SC_BG + j:SC_BG + j + 1]), reads=(b3tk,) + tuple(self.small.tks), writes=mt[0].tks)
            P.op("act", ACTF(mt[1].t[:, :], b4_[:, :], AF.Sigmoid, bias=small[:, SC_BG + 8 + j:SC_BG + 8 + j + 1]), reads=(b4tk,) + tuple(self.small.tks), writes=mt[1].tks)
            P.op("dve", TT(mt[2].t[:, :], mt[0].t[:, :], b1[:, :], ALU.mult), reads=tuple(mt[0].tks) + (b1tk,), writes=mt[2].tks)
            P.op("dve", TT(mt[3].t[:, :], mt[1].t[:, :], b2[:, :], ALU.mult), reads=tuple(mt[1].tks) + (b2tk,), writes=mt[3].tks)
            P.op("pool", TT(mgT.t[:, j, :], mt[2].t[:, :], mt[3].t[:, :], ALU.add), reads=tuple(mt[2].tks) + tuple(mt[3].tks), writes=(mgT.tks[j],))
        wo = [self.W.get(BWO, 4), self.W.get(BWO + 4, 4)]
        for s in range(NSUB):
            bk2 = [self.nb(), self.nb()]
            for h in range(2):
                P.op("pe", [MM(bk2[h][0][:, :], mgT.t[:, kc, s * 128:(s + 1) * 128], wo[h][0][:, kc * 512:(kc + 1) * 512], kc == 0, kc == 7) for kc in range(8)],
                     reads=tuple(mgT.tks) + (wo[h][1],), writes=(bk2[h][1],))
            self.ln_math(s, bk2, 1.0, None, r0, "h2")
        for s in range(NSUB):
            self.ln_post(s, self.ffnT)


def _fm_chunks(W, c0, ncc):
    K = W.shape[0]
    nk = K // 128
    sub = W[:, c0:c0 + ncc * 128].reshape(nk, 128, ncc, 128)
    return np.ascontiguousarray(sub.transpose(2, 1, 0, 3)).reshape(ncc, 128, nk * 128)


def _tm_block(W, c0, n):
    K = W.shape[0]
    nk = K // 128
    sub = W[:, c0:c0 + n].reshape(nk, 128, n)
    return np.ascontiguousarray(sub.transpose(1, 0, 2)).reshape(128, nk * n)


def pack_weights(inp):
    wp = np.zeros((NBLK, 128, 1024), np.float32)
    for fi, pre in enumerate(("ffn1", "ffn2")):
        wp[BG[fi]:BG[fi] + 22] = _fm_chunks(inp[pre + "_w_gate"], 0, 22)
        wp[BU[fi]:BU[fi] + 22] = _fm_chunks(inp[pre + "_w_up"], 0, 22)
        wp[BD[fi]:BD[fi] + 22] = inp[pre + "_w_down"].reshape(22, 128, 1024)
    win = inp["w_in"]
    wp[BIN + 0:BIN + 8] = _fm_chunks(win, OFF_Q, 8)
    wp[BIN + 8:BIN + 16] = _fm_chunks(win, OFF_K, 8)
    for vb in range(2):
        wp[BIN + 16 + 4 * vb:BIN + 20 + 4 * vb] = _tm_block(win, OFF_V + vb * 512, 512).reshape(128, 4, 1024).transpose(1, 0, 2)
    for g in range(NG):
        b = BIN + 24 + 6 * g
        wp[b:b + 2] = _tm_block(win, OFF_Z + g * 256, 256).reshape(128, 2, 1024).transpose(1, 0, 2)
        wp[b + 2:b + 4] = _fm_chunks(win, OFF_X + g * 256, 2)
        wp[b + 4:b + 5] = _fm_chunks(win, OFF_B + g * 128, 1)
        wp[b + 5:b + 6] = _fm_chunks(win, OFF_C + g * 128, 1)
    for j in range(8):
        wp[BIN + 72 + 2 * j] = _fm_chunks(win, OFF_G + j * 128, 1)[0]
        wp[BIN + 72 + 2 * j + 1] = _fm_chunks(win, OFF_G + 1024 + j * 128, 1)[0]
    wp[BDT, :, 0:256] = _tm_block(win, OFF_DT, 32)
    so = _fm_chunks(inp["w_ssm_out"], 0, 8)
    wp[BSO:BSO + 16] = so.reshape(8, 128, 2, 1024).transpose(0, 2, 1, 3).reshape(16, 128, 1024)
    wp[BAO:BAO + 8] = _fm_chunks(inp["w_att_out"], 0, 8)
    for hb in range(2):
        wp[BWO + 4 * hb:BWO + 4 * hb + 4] = _tm_block(inp["w_o"], hb * 512, 512).reshape(128, 4, 1024).transpose(1, 0, 2)
    return wp


def make_consts():
    c = np.zeros((128, CST_COLS), np.float32)
    i = np.arange(128)
    c[:, C_ID:C_ID + 128] = np.eye(128, dtype=np.float32)
    c[:, C_U:C_U + 128] = (i[:, None] <= i[None, :]).astype(np.float32)
    c[:, C_SL:C_SL + 128] = (i[:, None] > i[None, :]).astype(np.float32)
    c[:, C_ONE:C_ONE + 128] = 1.0
    qb = np.arange(8)[:, None]
    j = np.arange(8)[None, :]
    c[:, C_FUT:C_FUT + 64] = np.where(j >= qb, -1e30, 0.0).astype(np.float32).reshape(1, 64)
    c[:, C_VAL:C_VAL + 64] = (j < qb).astype(np.float32).reshape(1, 64)
    c[:, C_OWN:C_OWN + 64] = (j == qb).astype(np.float32).reshape(1, 64)
    oh = np.zeros((128, 8, 128), np.float32)
    for k in range(8):
        oh[k, k, :] = 1.0
    c[:, C_OH:C_OH + 1024] = oh.reshape(128, 1024)
    c[:, C_NH] = -0.5
    c[:, C_NH + 1] = RMS_EPS
    return c


def make_small(inp):
    s = np.zeros((128, SC_COLS), np.float32)
    s[:, SC_CW:SC_CW + 128] = inp["conv_w"].reshape(4, 32, 128).transpose(2, 1, 0).reshape(128, 128)
    s[:, SC_CB:SC_CB + 32] = inp["conv_b"].reshape(32, 128).T
    s[:, SC_NW:SC_NW + 16] = inp["ssm_norm_w"].reshape(16, 128).T
    s[:, SC_BG:SC_BG + 16] = inp["b_gate"].reshape(16, 128).T
    s[:, SC_DTB:SC_DTB + 32] = inp["dt_bias"][None, :]
    s[:, SC_ALOG:SC_ALOG + 32] = inp["a_log"][None, :]
    s[:, SC_DSK:SC_DSK + 32] = inp["d_skip"][None, :]
    return s


def make_lnp(inp):
    l = np.zeros((3, 128, 2 * D), np.float32)
    gains = [inp["ln1_g"], inp["ln2_g"], inp["ln3_g"]]
    biases = [inp["ln1_b"], inp["ln2_b"], inp["ln3_b"]]
    for i in range(3):
        l[i, :, 0:D] = gains[i][None, :]
        l[i, :, D:] = biases[i][None, :]
    return l


_CACHE = {}


def run(inputs, ncores=NCORES, nseq=None, nseg_limit=None, dbg=None):
    inp = {k: np.asarray(v, dtype=np.float32) for k, v in inputs.items()}
    x = inp["x"]
    if nseq is None:
        nseq = x.shape[0] // ncores
    key = (nseq, nseg_limit, dbg)
    if key not in _CACHE:
        _CACHE[key] = Builder(nseq, nseg_limit, dbg).build()
    nc = _CACHE[key]
    wp = pack_weights(inp)
    cst = make_consts()
    small = make_small(inp)
    lnp = make_lnp(inp)
    xs = x.reshape(-1, D)
    in_maps = []
    for c in range(ncores):
        in_maps.append({"x": np.ascontiguousarray(xs[c * nseq * SEQ:(c + 1) * nseq * SEQ]), "wpack": wp,
                        "cst_f32": cst, "smallc": small, "lnp": lnp})
    r = run_bass_kernel_spmd(nc, in_maps, core_ids=list(range(ncores)))
    out = np.concatenate([r.results[c]["out"] for c in range(ncores)], axis=0)
    if dbg:
        d = np.concatenate([r.results[c]["dbg_out"] for c in range(ncores)], axis=0)
        global LAST_DBGB
        LAST_DBGB = [r.results[c]["dbgb_out"] for c in range(ncores)]
        return out, d
    return out


def kernel(**inputs):
    out = run(inputs)
    return out.reshape(BATCH, SEQ, D).astype(np.float32)
```

```python
import numpy as np
import concourse.bass as bass
import concourse.mybir as mybir
from concourse.bass_utils import run_bass_kernel_spmd

F32 = mybir.dt.float32
BF16 = mybir.dt.bfloat16
AF = mybir.ActivationFunctionType
ALU = mybir.AluOpType
AX = mybir.AxisListType

D = 1024
DFF = 2816
SEQ = 2048
BATCH = 32
NCORES = 8
TS = 512
NSUB = 4
DIN = 2048
NH = 32
NG = 8
ALPHA = 2.0 ** 0.25
LN_EPS = 1e-5
RMS_EPS = 1e-5
IN_COLS = 11296
OFF_Z, OFF_X, OFF_B, OFF_C, OFF_DT, OFF_Q, OFF_K, OFF_V, OFF_G = 0, 2048, 4096, 5120, 6144, 6176, 7200, 8224, 9248
NEG = -30000.0
ATT_SCALE = 1.0 / (128.0 ** 0.5)

BG = [0, 66]
BU = [22, 88]
BD = [44, 110]
BIN = 132
BDT = 220
BSO = 221
BAO = 237
BWO = 245
NBLK = 253

C_ID, C_U, C_SL, C_ONE, C_FUT, C_VAL, C_OWN, C_OH, C_NH = 0, 128, 256, 384, 512, 576, 640, 704, 1728
CST_COLS = 1792
CB_ID, CB_ONE, CB_OH = 0, 128, 256
CSTB_COLS = 1280
SC_CW, SC_CB, SC_NW, SC_BG, SC_DTB, SC_ALOG, SC_DSK = 0, 128, 160, 176, 192, 224, 256
SC_COLS = 288


class Tk:
    __slots__ = ("name", "w", "r", "al", "dsem", "dcnt", "excl", "last")

    def __init__(self, name, excl=False):
        self.name = name
        self.last = 0
        self.excl = excl
        self.w = None
        self.r = {}
        self.al = []
        self.dsem = None
        self.dcnt = 0


class Eng:
    def __init__(self, name):
        self.name = name
        self.ops = []
        self.sem = None
        self.count = 0
        self.known = {}


class Prog:
    def __init__(self, nc, dry):
        self.nc = nc
        self.dry = dry
        self.E = {n: Eng(n) for n in ("pe", "act", "dve", "pool", "sp")}
        self.nsem = 0
        self.ninst = 0
        self.dma_tks = []
        if not dry:
            self.new_epoch(0)

    def _newsem(self, name):
        s = self.nc.alloc_semaphore(name)
        self.nsem += 1
        return s

    def new_epoch(self, ep):
        if self.dry:
            return
        for n, e in self.E.items():
            e.sem = self._newsem("p_%s_%d" % (n, ep))
            e.count = 0

    def _deps(self, reads, writes, own=None):
        deps = {}

        def add(tok):
            if tok is None:
                return
            s, v = tok
            cur = deps.get(id(s))
            if cur is None or cur[1] < v:
                deps[id(s)] = (s, v)

        for t in reads:
            add(t.w)
            if t.excl:
                for tok in t.r.values():
                    if tok[0] is not own:
                        add(tok)
        for t in writes:
            add(t.w)
            for tok in t.r.values():
                add(tok)
            for a in t.al:
                add(a.w)
                for tok in a.r.values():
                    add(tok)
        return deps

    def _emit_waits(self, e, deps, excl_only=False):
        for s, v in deps.values():
            if s is e.sem and (e.name == "pe" or excl_only):
                continue
            if e.known.get(id(s), 0) >= v:
                continue
            e.known[id(s)] = v
            e.ops.append(lambda eng, s=s, v=v: eng.wait_ge(s, v))

    def op(self, en, fn, reads=(), writes=()):
        if self.dry:
            return
        e = self.E[en]
        self._emit_waits(e, self._deps(reads, writes, e.sem))
        e.count += 1
        sem, cnt = e.sem, e.count
        if isinstance(fn, (list, tuple)):
            fns = list(fn)
            for f in fns[:-1]:
                e.ops.append(f)
            last = fns[-1]
            e.ops.append(lambda eng, f=last, sem=sem: f(eng).then_inc(sem, 1))
            self.ninst += len(fns)
        else:
            e.ops.append(lambda eng, f=fn, sem=sem: f(eng).then_inc(sem, 1))
            self.ninst += 1
        tok = (sem, cnt)
        self.opn = getattr(self, "opn", 0) + 1
        for t in reads:
            t.r[id(sem)] = tok
            t.last = self.opn
        for t in writes:
            t.w = tok
            t.r = {}
            t.last = self.opn

    def dma(self, qn, out_ap, in_ap, reads=(), writes=(), semtk=None):
        if self.dry:
            return
        e = self.E[qn]
        self._emit_waits(e, self._deps(reads, writes, e.sem))
        tk = semtk if semtk is not None else (writes[0] if writes else reads[0])
        if tk.dsem is None:
            tk.dsem = self._newsem("d_" + tk.name)
            self.dma_tks.append(tk)
        sem = tk.dsem
        tk.dcnt += 1
        val = 16 * tk.dcnt
        e.ops.append(lambda eng, sem=sem, o=out_ap, i=in_ap: eng.dma_start(out=o, in_=i).then_inc(sem, 16))
        self.ninst += 1
        tok = (sem, val)
        for t in reads:
            t.r[id(sem)] = tok
        for t in writes:
            t.w = tok
            t.r = {}

    def final_wait(self, en, tks):
        if self.dry:
            return
        e = self.E[en]
        for o in self.E.values():
            if o.count > 0 and o is not e:
                e.ops.append(lambda eng, s=o.sem, v=o.count: eng.wait_ge(s, v))
        for t in self.dma_tks:
            e.ops.append(lambda eng, s=t.dsem, v=16 * t.dcnt: eng.wait_ge(s, v))
        for t in tks:
            if t.w is not None:
                s, v = t.w
                e.ops.append(lambda eng, s=s, v=v: eng.wait_ge(s, v))


def MM(out, lhsT, rhs, start=True, stop=True):
    return lambda e: e.matmul(out, lhsT=lhsT, rhs=rhs, start=start, stop=stop)


def TR(out, in_, ident):
    return lambda e: e.transpose(out, in_, ident)


def ACTF(out, in_, func, bias=None, scale=None, accum_out=None):
    kw = {}
    if bias is not None:
        kw["bias"] = bias
    if scale is not None:
        kw["scale"] = scale
    if accum_out is not None:
        kw["accum_out"] = accum_out
    return lambda e: e.activation(out=out, in_=in_, func=func, **kw)


def TT(out, in0, in1, op):
    return lambda e: e.tensor_tensor(out=out, in0=in0, in1=in1, op=op)


def TSC(out, in0, s1, s2, op0, op1=None):
    if op1 is None:
        return lambda e: e.tensor_scalar(out=out, in0=in0, scalar1=s1, scalar2=None, op0=op0)
    return lambda e: e.tensor_scalar(out=out, in0=in0, scalar1=s1, scalar2=s2, op0=op0, op1=op1)


def STT(out, in0, scalar, in1, op0, op1):
    return lambda e: e.scalar_tensor_tensor(out=out, in0=in0, scalar=scalar, in1=in1, op0=op0, op1=op1)


def CP(out, in_):
    return lambda e: e.tensor_copy(out=out, in_=in_)


class Mem:
    def __init__(self, nc):
        self.nc = nc
        self.base = ((nc.sbuf_base + 63) // 64) * 64
        self.top = nc.sbuf_top
        self.recs = []

    def at(self, name, shape, dt, off, tks):
        esz = 4 if dt == F32 else 2
        n = 1
        for s in shape[1:]:
            n *= s
        nbytes = n * esz
        assert off % 32 == 0, (name, off)
        assert self.base + off + nbytes <= self.top, (name, off, nbytes)
        t = self.nc.alloc_sbuf_tensor_at(name, list(shape), dt, offset=self.base + off)
        self.recs.append((off, off + nbytes, list(tks)))
        return t

    def finalize(self):
        n = len(self.recs)
        for i in range(n):
            a0, a1, ta = self.recs[i]
            for j in range(i + 1, n):
                b0, b1, tb = self.recs[j]
                if a0 < b1 and b0 < a1:
                    for x in ta:
                        for y in tb:
                            if x is not y:
                                x.al.append(y)
                                y.al.append(x)


def interleave(gens):
    gens = list(gens)
    while gens:
        for g_ in list(gens):
            try:
                next(g_)
            except StopIteration:
                gens.remove(g_)


class Buf:
    def __init__(self, t, tks):
        self.t = t
        self.tks = tks


class WStream:
    def __init__(self, P, slots, sched, wscr, wsc_tk, builder=None):
        self.builder = builder
        self.P = P
        self.slots = slots
        self.sched = sched
        self.pos = 0
        self.issued = 0
        self.wscr = wscr
        self.wsc_tk = wsc_tk

    def _issue(self, k):
        tens, tk = self.slots[k % len(self.slots)]
        b0, nb = self.sched[k]
        dst = tens[:, 0:nb * 1024].rearrange("p (c e) -> p c e", c=nb)
        src = self.wscr[b0:b0 + nb, :, :].rearrange("c p e -> p c e")
        ri = self.builder.region_of(b0)
        self.builder.ensure_region(ri)
        self.P.dma("sp", dst, src, reads=(self.builder.region_tk[ri],), writes=(tk,))

    def get(self, b0, nb):
        P = self.P
        k = self.pos
        self.pos += 1
        if P.dry:
            self.sched.append((b0, nb))
            return self.slots[k % len(self.slots)]
        assert self.sched[k] == (b0, nb), (k, self.sched[k], b0, nb)
        ahead = len(self.slots) - 2
        while self.issued < min(len(self.sched), k + ahead + 1):
            self._issue(self.issued)
            self.issued += 1
        self.builder.pump(2)
        return self.slots[k % len(self.slots)]


class Builder:
    def __init__(self, nseq, nseg_limit=None, dbg=None, stop=""):
        self.stop = stop
        self.nseq = nseq
        self.nseg_total = nseq * (SEQ // TS)
        if nseg_limit is not None:
            self.nseg_total = min(self.nseg_total, nseg_limit)
        self.dbg = dbg
        self.ntok = nseq * SEQ

    def declare(self):
        nc = bass.Bass("TRN2", target_bir_lowering=False)
        self.nc = nc
        ntok = self.ntok
        self.x_d = nc.dram_tensor("x", [ntok, D], F32, kind="ExternalInput").ap()
        self.wpack_d = nc.dram_tensor("wpack", [NBLK, 128, 1024], F32, kind="ExternalInput").ap()
        self.cst_d = nc.dram_tensor("cst_f32", [128, CST_COLS], F32, kind="ExternalInput").ap()
        self.small_d = nc.dram_tensor("smallc", [128, SC_COLS], F32, kind="ExternalInput").ap()
        self.lnp_d = nc.dram_tensor("lnp", [3, 128, 2 * D], F32, kind="ExternalInput").ap()
        self.out_d = nc.dram_tensor("out", [ntok, D], F32, kind="ExternalOutput").ap()
        self.dbg_d = None
        if self.dbg:
            self.dbg_d = nc.dram_tensor("dbg_out", [ntok, D], F32, kind="ExternalOutput").ap()
        self.dbgb_d = None
        if self.dbg:
            self.dbgb_d = nc.dram_tensor("dbgb_out", [self.nseg_total, 128, 8192], BF16, kind="ExternalOutput").ap()
        self.wscr = nc.dram_tensor("wscr", [NBLK, 128, 1024], BF16, kind="Internal").ap()
        self.kt_s = nc.dram_tensor("kt_s", [8, 128, SEQ], BF16, kind="Internal").ap()
        self.v_s = nc.dram_tensor("v_s", [SEQ, D], BF16, kind="Internal").ap()

        M = Mem(nc)
        self.M = M
        off = [0]

        def per(name, shape, dt, ntk=1):
            esz = 4 if dt == F32 else 2
            n = 1
            for s in shape[1:]:
                n *= s
            nb = ((n * esz + 63) // 64) * 64
            tks = [Tk("%s_%d" % (name, i)) for i in range(ntk)]
            t = M.at(name, shape, dt, off[0], tks)
            off[0] += nb
            return Buf(t, tks)

        self.cst = per("cst", [128, CST_COLS], F32)
        self.cstb = per("cstb", [128, CSTB_COLS], BF16)
        self.small = per("small", [128, 512], F32)
        self.lnp = per("lnpb", [128, 2 * D], F32)
        self.wdt = per("wdt", [128, 256], BF16)
        self.wsl = [per("wslot%d" % i, [128, 4096], BF16) for i in range(4)]
        self.res = per("res", [128, NSUB, D], F32, NSUB)
        self.ffnT = per("ffnT", [128, 8, TS], BF16, NSUB)
        self.h1T = per("h1T", [128, 8, TS], BF16, NSUB)
        self.stat = [per("stat%d" % i, [128, 16], F32) for i in range(4)]
        self.abc = per("abc", [128, 32], F32)
        self.carry = per("carry", [128, 32, 3], F32, NG)
        self.kmT = per("kmT", [128, 8, 8], F32)
        self.kmTb = per("kmTb", [128, 8, 8], BF16)
        self.st = per("st", [128, DIN], F32, NG)
        arena0 = off[0]
        self.arena0 = arena0

        def ar(name, shape, dt, o, ntk=1):
            tks = [Tk("%s_%d" % (name, i)) for i in range(ntk)]
            t = M.at(name, shape, dt, arena0 + o, tks)
            return Buf(t, tks)

        o = 0
        self.xst = ar("xst", [128, NSUB, D], F32, o, NSUB)
        self.ynT = ar("ynT", [128, 16, TS], BF16, o, 16); o += 16384
        self.OT = ar("OT", [128, 8, TS], BF16, o, 8); o += 8192
        self.QT = ar("QT", [128, 8, TS], BF16, o, 8); o_qt = o; o += 8192
        self.mgT = ar("mgT", [128, 8, TS], BF16, o_qt, 8)
        at0 = o
        self.kbuf = [ar("kbuf%d" % i, [128, SEQ], BF16, o + i * 4096) for i in range(2)]; o += 8192
        self.vbuf = [ar("vbuf%d" % i, [128, 16, 128], BF16, o + i * 4096) for i in range(2)]; o += 8192
        stg_end = o
        self.PT = [ar("PT%d" % i, [128, TS], BF16, o + i * 1024) for i in range(2)]; o += 2048
        self.rinv = ar("rinv", [128, TS], F32, o); o += 2048
        self.gm = ar("gm", [128, 8, 8], F32, o); o += 256
        self.top8 = ar("top8", [128, 8, 8], F32, o); o += 256
        self.sel = ar("sel", [128, 8, 8], F32, o); o += 256
        self.biasq = ar("biasq", [128, 8, 8], BF16, o); o += 128
        o += 128
        at1 = o
        self.biasT = ar("biasT", [128, 8, TS], BF16, o); o += 8192
        ssd0 = o
        self.raw = ar("raw", [128, 4, 516], F32, o, 4); o += 8256
        self.xc = ar("xc", [128, 4, TS], BF16, o, 4); o_xc = o; o += 4096
        self.sz = ar("sz", [128, 4, 256], F32, o, 4); o += 4096
        self.xc2 = ar("xc2", [128, 4, TS], BF16, o, 4); o += 4096
        self.sz2 = ar("sz2", [128, 4, 256], F32, o, 4); o += 4096
        self.KTseg = ar("KTseg", [128, 8, TS], BF16, ssd0)
        self.Vseg = ar("Vseg", [128, 4, D], BF16, o_xc)
        names = ["dtu", "dtv", "da", "acum", "eac", "edec", "wdf", "wex"]
        self.dtt = {}
        for nme in names:
            self.dtt[nme] = ar(nme, [128, 4, 32], F32, o); o += 512

        def ckset(i, o):
            dct = {}
            dct["xtok"] = ar("xtok%d" % i, [128, 384], BF16, o); o += 768
            dct["xdt"] = ar("xdt%d" % i, [128, 256], BF16, o); o += 512
            dct["xw"] = ar("xw%d" % i, [128, 256], BF16, o); o += 512
            dct["xd"] = ar("xd%d" % i, [128, 256], F32, o); o += 1024
            dct["CBm"] = ar("CBm%d" % i, [128, 128], F32, o); o += 512
            dct["R1"] = ar("R1%d" % i, [128, 512], F32, o)
            dct["L"] = ar("L%d" % i, [128, 512], F32, o); o += 2048
            dct["sq"] = dct["L"]
            dct["MT"] = ar("MT%d" % i, [128, 512], BF16, o); o += 1024
            dct["t1"] = ar("t1%d" % i, [128, 256], F32, o); o += 1024
            dct["yn"] = ar("yn%d" % i, [128, 256], BF16, o); o += 512
            dct["ss"] = ar("ss%d" % i, [128, 16], F32, o); o += 64
            return dct, o
        self.ck = []
        for i in range(2):
            dct, o = ckset(i, o)
            self.ck.append(dct)
        oa = at0
        for i in range(2, 4):
            dct, oa = ckset(i, oa)
            self.ck.append(dct)
        assert oa <= at1, (oa, at1)
        self.stbc = ar("stbc", [128, 4, 256], BF16, o); o += 2048
        ssd_end = o
        self.mt = [ar("mtmp%d" % i, [128, TS], F32, o + i * 2048) for i in range(4)]; o += 8192
        mix_end = o
        o = ssd0
        self.HT = ar("HT", [128, 22, TS], BF16, o, 22); o += 22528
        self.tmpA = [ar("tmpA%d" % i, [128, TS], F32, o + i * 2048) for i in range(2)]; o += 4096
        assert o <= ssd_end + 8192, (o, ssd_end)
        o = 0
        self.stgf = [ar("stgf%d" % i, [128, 4096], F32, o + i * 16384) for i in range(2)]; o += 32768
        self.stgb = [ar("stgb%d" % i, [128, 4096], BF16, o + i * 8192) for i in range(2)]; o += 16384
        assert o <= stg_end, (o, stg_end)
        assert M.base + arena0 + mix_end <= M.top, (M.base + arena0 + mix_end, M.top)
        M.finalize()
        self.banks = [nc.alloc_psum_tensor("bank%d" % i, [128, 512], F32) for i in range(8)]

    def build(self):
        self.declare()
        sched = []
        for dry in (True, False):
            self.P = Prog(self.nc, dry)
            self.bank_tk = [Tk("bank%d" % i, excl=True) for i in range(8)]
            self.bank_i = 0
            self.ring = list(range(8))
            self.held = set()
            self.dbgb_tk = Tk("dbgbd")
            self.wsc_tk = Tk("wscr")
            self.kts_tk = Tk("kts")
            self.vs_tk = Tk("vs")
            self.out_tk = Tk("outd")
            self.dbg_tk = Tk("dbgd")
            self.rot = {}
            self.W = WStream(self.P, [(b.t, b.tks[0]) for b in self.wsl], sched, self.wscr, self.wsc_tk, builder=self)
            self.wdt_loaded = False
            if not dry:
                self._reset_tks()
            self.program()
        self.emit()
        return self.nc

    def _reset_tks(self):
        for (_, _, tks) in self.M.recs:
            for t in tks:
                t.w = None
                t.r = {}
                t.dsem = None
                t.dcnt = 0

    def emit(self):
        nc, P = self.nc, self.P
        with nc.Block() as block:
            @block.tensor
            def _(e):
                for th in P.E["pe"].ops:
                    th(e)

            @block.scalar
            def _(e):
                for th in P.E["act"].ops:
                    th(e)

            @block.vector
            def _(e):
                for th in P.E["dve"].ops:
                    th(e)

            @block.gpsimd
            def _(e):
                for th in P.E["pool"].ops:
                    th(e)

            @block.sync
            def _(e):
                for th in P.E["sp"].ops:
                    th(e)

    def nb(self, hold=False):
        cand = [j for j in self.ring if j not in self.held]
        i = min(cand, key=lambda j: (self.bank_tk[j].last, j))
        self.bank_tk[i].last = getattr(self.P, "opn", 0) + 0.5
        if hold:
            self.held.add(i)
        return self.banks[i], self.bank_tk[i]

    def unhold(self, btk):
        self.held.discard(self.bank_tk.index(btk))

    def ev(self):
        return "act"

    def program(self):
        P = self.P
        cst, cstb = self.cst, self.cstb
        P.dma("sp", cst.t[:, :], self.cst_d[:, :], writes=cst.tks)
        P.dma("sp", self.small.t[:, 0:SC_COLS], self.small_d[:, :], writes=self.small.tks)
        P.op("dve", CP(cstb.t[:, CB_ID:CB_ID + 128], cst.t[:, C_ID:C_ID + 128]), reads=cst.tks, writes=cstb.tks)
        P.op("dve", CP(cstb.t[:, CB_ONE:CB_ONE + 128], cst.t[:, C_ONE:C_ONE + 128]), reads=cst.tks, writes=cstb.tks)
        P.op("dve", CP(cstb.t[:, CB_OH:CB_OH + 1024], cst.t[:, C_OH:C_OH + 1024]), reads=cst.tks, writes=cstb.tks)
        P.op("act", ACTF(self.abc.t[:, :], self.small.t[:, SC_ALOG:SC_ALOG + 32], AF.Exp), reads=self.small.tks, writes=self.abc.tks)
        P.op("dve", TSC(self.abc.t[:, :], self.abc.t[:, :], -1.0, None, ALU.mult), reads=self.abc.tks, writes=self.abc.tks)
        self.prologue()
        nps = SEQ // TS
        for seg in range(self.nseg_total):
            if seg > 0 and seg % 2 == 0:
                P.new_epoch(seg // 2)
            self.segment(seg, seg // nps, seg % nps)
        P.final_wait("sp", [self.out_tk, self.dbg_tk, self.dbgb_tk])

    REGIONS = [(0, 44), (44, 66), (132, 221), (221, 253), (66, 110), (110, 132)]

    def region_of(self, b):
        for i, (lo, hi) in enumerate(self.REGIONS):
            if lo <= b < hi:
                return i
        raise ValueError(b)

    def conv_init(self):
        self.conv_steps = []
        for ri, (lo, hi) in enumerate(self.REGIONS):
            for b0 in range(lo, hi, 4):
                self.conv_steps.append((ri, b0, min(4, hi - b0)))
        self.conv_pos = 0
        self.region_tk = [Tk("wreg%d" % i) for i in range(len(self.REGIONS))]

    def pump(self, n):
        P = self.P
        if P.dry:
            return
        while n > 0 and self.conv_pos < len(self.conv_steps):
            k = self.conv_pos
            ri, b0, nb = self.conv_steps[k]
            self.conv_pos += 1
            n -= 1
            sf, sbb = self.stgf[k % 2], self.stgb[k % 2]
            m = nb * 1024
            P.dma("pool", sf.t[:, 0:m].rearrange("p (c e) -> p c e", c=nb),
                  self.wpack_d[b0:b0 + nb, :, :].rearrange("c p e -> p c e"), writes=sf.tks)
            if k % 2 == 0:
                P.op("act", ACTF(sbb.t[:, 0:m], sf.t[:, 0:m], AF.Copy), reads=sf.tks, writes=sbb.tks)
            else:
                P.op("dve", CP(sbb.t[:, 0:m], sf.t[:, 0:m]), reads=sf.tks, writes=sbb.tks)
            P.dma("pool", self.wscr[b0:b0 + nb, :, :].rearrange("c p e -> p c e"),
                  sbb.t[:, 0:m].rearrange("p (c e) -> p c e", c=nb), reads=sbb.tks, writes=(self.region_tk[ri],))

    def ensure_region(self, ri):
        if self.P.dry:
            return
        while self.conv_pos < len(self.conv_steps) and self.conv_steps[self.conv_pos][0] <= ri:
            self.pump(1)

    def prologue(self):
        self.conv_init()
        self.ensure_region(1)

    def segment(self, seg, sq, t):
        P = self.P
        r0 = seg * TS
        res = self.res
        if seg == 0:
            self.load_x(0)
            self.x_front()
            self.x_res()
        self.ffn(0, self.ffnT, 0, self.h1T, None, r0, mid_hook=(self.x_res if seg > 0 else None))
        if self.dbg == "h1":
            return
        self.mixer(seg, sq, t, r0)
        if self.stop:
            return
        nxt = seg + 1 if seg + 1 < self.nseg_total else None
        self.ffn(1, self.ffnT, 2, None, self.out_d, r0, next_seg=nxt)

    def load_x(self, seg):
        r0 = seg * TS
        self.P.dma("sp", self.xst.t[:, :, :], self.x_d[r0:r0 + TS, :].rearrange("(s p) d -> p s d", p=128), writes=self.xst.tks)

    def x_front(self):
        for s in range(NSUB):
            self.transpose_sub(s, self.ffnT, src=self.xst)

    def x_res(self):
        for s in range(NSUB):
            self.P.op("act", ACTF(self.res.t[:, s, :], self.xst.t[:, s, :], AF.Copy, scale=ALPHA), reads=(self.xst.tks[s],), writes=(self.res.tks[s],))

    def load_ln(self, idx):
        self.P.dma("sp", self.lnp.t[:, :], self.lnp_d[idx, :, :], writes=self.lnp.tks)

    def scale_res(self, s):
        res = self.res
        self.P.op("act", ACTF(res.t[:, s, :], res.t[:, s, :], AF.Copy, scale=ALPHA), reads=(res.tks[s],), writes=(res.tks[s],))

    def transpose_sub(self, s, dstT, src=None):
        P = self.P
        res = src if src is not None else self.res
        ident = self.cst.t[:, C_ID:C_ID + 128]
        for half in range(2):
            bk, btk = self.nb()
            grp = [TR(bk[:, j * 128:(j + 1) * 128], res.t[:, s, (half * 4 + j) * 128:(half * 4 + j + 1) * 128], ident) for j in range(4)]
            P.op("pe", grp, reads=(res.tks[s],) + tuple(self.cst.tks), writes=(btk,))
            dst = dstT.t[:, half * 4:half * 4 + 4, s * 128:(s + 1) * 128]
            src = bk[:, :].rearrange("p (j q) -> p j q", j=4)
            en = self.ev()
            if en == "act":
                P.op("act", ACTF(dst, src, AF.Copy), reads=(btk,), writes=(dstT.tks[s],))
            else:
                P.op("dve", CP(dst, src), reads=(btk,), writes=(dstT.tks[s],))

    def ffn(self, fi, aT, lnidx, outT, out_d, r0, mid_hook=None, next_seg=None):
        P = self.P
        HT = self.HT
        for b in range(6):
            ncc = 4 if b < 5 else 2
            gs, gtk = self.W.get(BG[fi] + 4 * b, ncc)
            us, utk = self.W.get(BU[fi] + 4 * b, ncc)
            for cc in range(ncc):
                fj = 4 * b + cc
                bg, bgtk = self.nb()
                bu, butk = self.nb()
                P.op("pe", [MM(bg[:, :], gs[:, cc * 1024 + kc * 128:cc * 1024 + (kc + 1) * 128], aT.t[:, kc, :], kc == 0, kc == 7) for kc in range(8)],
                     reads=(gtk,) + tuple(aT.tks), writes=(bgtk,))
                P.op("pe", [MM(bu[:, :], us[:, cc * 1024 + kc * 128:cc * 1024 + (kc + 1) * 128], aT.t[:, kc, :], kc == 0, kc == 7) for kc in range(8)],
                     reads=(utk,) + tuple(aT.tks), writes=(butk,))
                tA = self.tmpA[fj % 2]
                P.op("act", ACTF(tA.t[:, :], bg[:, :], AF.Silu), reads=(bgtk,), writes=tA.tks)
                P.op("dve", TT(HT.t[:, fj, :], tA.t[:, :], bu[:, :], ALU.mult), reads=tuple(tA.tks) + (butk,), writes=(HT.tks[fj],))
        self.load_ln(lnidx)
        if mid_hook is not None:
            mid_hook()
        if next_seg is not None:
            self.load_x(next_seg)
        for ps in range(2):
            bks = [[self.nb() for h in range(2)] for s2 in range(2)]
            for blk in range(6):
                nfc = 4 if blk < 5 else 2
                ds, dtk = self.W.get(BD[fi] + 4 * blk, nfc)
                grp = []
                for fcl in range(nfc):
                    fc = 4 * blk + fcl
                    for s2 in range(2):
                        s = 2 * ps + s2
                        for h in range(2):
                            grp.append(MM(bks[s2][h][0][:, :], HT.t[:, fc, s * 128:(s + 1) * 128],
                                          ds[:, fcl * 1024 + h * 512:fcl * 1024 + (h + 1) * 512], fc == 0, fc == 21))
                P.op("pe", grp, reads=(dtk,) + tuple(HT.tks[4 * blk:4 * blk + nfc]),
                     writes=tuple(bks[s2][h][1] for s2 in range(2) for h in range(2)))
            tag = "h1" if fi == 0 else "out"
            for s2 in range(2):
                self.ln_math(2 * ps + s2, bks[s2], 0.5, out_d, r0, tag)
            if ps == 0 and next_seg is not None:
                self.x_front()
            if ps == 1:
                for s in range(4):
                    self.ln_post(s, outT)

    def ln_math(self, s, bk2, scale, out_d, r0, tag):
        P = self.P
        res, lnp = self.res, self.lnp
        rt = res.tks[s]
        st = self.stat[s]
        for h in range(2):
            P.op("dve", STT(res.t[:, s, h * 512:(h + 1) * 512], bk2[h][0][:, :], scale, res.t[:, s, h * 512:(h + 1) * 512], ALU.mult, ALU.add),
                 reads=(bk2[h][1], rt), writes=(rt,))
        for h in range(2):
            P.op("dve", lambda e, h=h: e.bn_stats(out=st.t[:, 6 * h:6 * h + 6], in_=res.t[:, s, h * 512:(h + 1) * 512]), reads=(rt,), writes=st.tks)
        P.op("dve", lambda e: e.bn_aggr(out=st.t[:, 12:14], in_=st.t[:, 0:12]), reads=st.tks, writes=st.tks)
        P.op("pool", TSC(st.t[:, 14:15], st.t[:, 13:14], LN_EPS, None, ALU.add), reads=st.tks, writes=st.tks)
        P.op("pool", TT(st.t[:, 15:16], st.t[:, 14:15], self.cst.t[:, C_NH:C_NH + 1], ALU.pow), reads=tuple(st.tks) + tuple(self.cst.tks), writes=st.tks)
        P.op("dve", TSC(res.t[:, s, :], res.t[:, s, :], st.t[:, 12:13], st.t[:, 15:16], ALU.subtract, ALU.mult), reads=(rt,) + tuple(st.tks), writes=(rt,))
        P.op("dve", TT(res.t[:, s, :], res.t[:, s, :], lnp.t[:, 0:D], ALU.mult), reads=(rt,) + tuple(lnp.tks), writes=(rt,))
        P.op("pool", TT(res.t[:, s, :], res.t[:, s, :], lnp.t[:, D:2 * D], ALU.add), reads=(rt,) + tuple(lnp.tks), writes=(rt,))
        if self.dbg == tag and tag != "out":
            P.dma("sp", self.dbg_d[r0 + s * 128:r0 + (s + 1) * 128, :], res.t[:, s, :], reads=(rt,), writes=(self.dbg_tk,))
        if out_d is not None:
            P.dma("pool", out_d[r0 + s * 128:r0 + (s + 1) * 128, :], res.t[:, s, :], reads=(rt,), writes=(self.out_tk,))

    def ln_post(self, s, outT):
        if outT is not None:
            self.transpose_sub(s, outT)
            self.scale_res(s)

    def dump_bf(self, seg, ap, n, tks):
        self.P.dma("sp", self.dbgb_d[seg, :, 0:n], ap, reads=tuple(tks), writes=(self.dbgb_tk,))

    def evac(self, dst, src, btk, wtks, en=None):
        en = en or self.ev()
        if en == "act":
            self.P.op("act", ACTF(dst, src, AF.Copy), reads=(btk,), writes=tuple(wtks))
        else:
            self.P.op(en, CP(dst, src), reads=(btk,), writes=tuple(wtks))

    def mixer(self, seg, sq, t, r0):
        P = self.P
        self.ring = list(range(8))
        if t == 0:
            P.op("pool", lambda e: e.memset(self.st.t[:, :], 0.0), writes=self.st.tks)
            P.op("pool", lambda e: e.memset(self.carry.t[:, :, :], 0.0), writes=self.carry.tks)
            P.op("pool", lambda e: e.memset(self.kmT.t[:, :, :], 0.0), writes=self.kmT.tks)
            P.op("pool", lambda e: e.memset(self.kmTb.t[:, :, :], 0.0), writes=self.kmTb.tks)
        stop = self.stop
        self.qkv(seg, t)
        if stop == "qkv":
            return
        self.gate_bias(t)
        if stop == "gate":
            return
        self.ssd(seg, t)
        if self.dbg == "yn":
            self.dump_bf(seg, self.ynT.t[:, :, :].rearrange("p a b -> p (a b)"), 8192, self.ynT.tks)
        if stop == "ssd":
            return
        self.attention(seg, t)
        if self.dbg == "ya":
            self.dump_bf(seg, self.OT.t[:, :, :].rearrange("p a b -> p (a b)"), 4096, self.OT.tks)
        if stop == "att":
            return
        self.merge(seg, t, r0)

    def qkv(self, seg, t):
        P = self.P
        h1T, QT, KTseg, Vseg = self.h1T, self.QT, self.KTseg, self.Vseg
        for hq in range(2):
            qs, qtk = self.W.get(BIN + 4 * hq, 4)
            for cc in range(4):
                h = 4 * hq + cc
                bk, btk = self.nb()
                P.op("pe", [MM(bk[:, :], qs[:, cc * 1024 + kc * 128:cc * 1024 + (kc + 1) * 128], h1T.t[:, kc, :], kc == 0, kc == 7) for kc in range(8)],
                     reads=(qtk,) + tuple(h1T.tks), writes=(btk,))
                self.evac(QT.t[:, h, :], bk[:, :], btk, (QT.tks[h],))
        for hk in range(2):
            ks, ktk = self.W.get(BIN + 8 + 4 * hk, 4)
            for cc in range(4):
                h = 4 * hk + cc
                bk, btk = self.nb()
                P.op("pe", [MM(bk[:, :], ks[:, cc * 1024 + kc * 128:cc * 1024 + (kc + 1) * 128], h1T.t[:, kc, :], kc == 0, kc == 7) for kc in range(8)],
                     reads=(ktk,) + tuple(h1T.tks), writes=(btk,))
                for b in range(2):
                    P.op("act", ACTF(KTseg.t[:, h, b * 256:(b + 1) * 256], bk[:, b * 256:(b + 1) * 256], AF.Identity,
                                     accum_out=self.kmT.t[:, h, 2 * t + b:2 * t + b + 1]), reads=(btk,), writes=tuple(KTseg.tks) + tuple(self.kmT.tks))
        for vb in range(2):
            vs, vtk = self.W.get(BIN + 16 + 4 * vb, 4)
            for s in range(NSUB):
                bk, btk = self.nb()
                P.op("pe", [MM(bk[:, :], h1T.t[:, kc, s * 128:(s + 1) * 128], vs[:, kc * 512:(kc + 1) * 512], kc == 0, kc == 7) for kc in range(8)],
                     reads=(vtk, h1T.tks[s]), writes=(btk,))
                self.evac(Vseg.t[:, s, vb * 512:(vb + 1) * 512], bk[:, :], btk, Vseg.tks)
        if True:
            P.dma("sp", self.kt_s[:, :, t * TS:(t + 1) * TS].rearrange("h p q -> p h q"), KTseg.t[:, :, :], reads=KTseg.tks, writes=(self.kts_tk,))
        if True:
            P.dma("sp", self.v_s[t * TS:(t + 1) * TS, :].rearrange("(s p) d -> p s d", p=128), Vseg.t[:, :, :], reads=Vseg.tks, writes=(self.vs_tk,))
        P.op("dve", CP(self.kmTb.t[:, :, 2 * t:2 * t + 2], self.kmT.t[:, :, 2 * t:2 * t + 2]), reads=self.kmT.tks, writes=self.kmTb.tks)

    def gate_bias(self, t):
        P = self.P
        cst, cstb = self.cst.t, self.cstb.t
        QT, gm, top8, sel, biasq, biasT = self.QT, self.gm, self.top8, self.sel, self.biasq, self.biasT
        ident_b = cstb[:, CB_ID:CB_ID + 128]
        for s in range(NSUB):
            qb = 2 * t + s // 2
            bk, btk = self.nb()
            P.op("pe", [MM(bk[:, h * 8:(h + 1) * 8], QT.t[:, h, s * 128:(s + 1) * 128], self.kmTb.t[:, h, :]) for h in range(8)],
                 reads=tuple(QT.tks) + tuple(self.kmTb.tks), writes=(btk,))
            bc = lambda c0: cst[:, c0 + qb * 8:c0 + qb * 8 + 8].unsqueeze(1).to_broadcast([128, 8, 8])
            P.op("dve", TT(gm.t[:, :, :], bk[:, 0:64].rearrange("p (h j) -> p h j", h=8), bc(C_FUT), ALU.add), reads=(btk,) + tuple(self.cst.tks), writes=gm.tks)
            for h in range(8):
                P.op("dve", lambda e, h=h: e.max(out=top8.t[:, h, :], in_=gm.t[:, h, :]), reads=gm.tks, writes=top8.tks)
            P.op("dve", TT(sel.t[:, :, :], gm.t[:, :, :], top8.t[:, :, 2:3].to_broadcast([128, 8, 8]), ALU.is_ge), reads=tuple(gm.tks) + tuple(top8.tks), writes=sel.tks)
            P.op("dve", TT(sel.t[:, :, :], sel.t[:, :, :], bc(C_VAL), ALU.mult), reads=sel.tks, writes=sel.tks)
            P.op("dve", TT(sel.t[:, :, :], sel.t[:, :, :], bc(C_OWN), ALU.add), reads=sel.tks, writes=sel.tks)
            P.op("dve", TSC(biasq.t[:, :, :], sel.t[:, :, :], -NEG, NEG, ALU.mult, ALU.add), reads=sel.tks, writes=biasq.tks)
            bk2, btk2 = self.nb()
            b2 = bk2[:, :].bitcast(BF16)
            P.op("pe", [TR(b2[0:8, h * 128:(h + 1) * 128], biasq.t[:, h, :], ident_b) for h in range(8)], reads=tuple(biasq.tks) + tuple(self.cstb.tks), writes=(btk2,))
            P.op("act", ACTF(biasT.t[0:8, :, s * 128:(s + 1) * 128], b2[0:8, 0:1024].rearrange("p (h q) -> p h q", h=8), AF.Copy), reads=(btk2,), writes=biasT.tks)

    def ckpt(self, n):
        return False

    def ssd(self, seg, t):
        P = self.P
        cst, cstb, small = self.cst.t, self.cstb.t, self.small.t
        h1T, dtt = self.h1T, self.dtt
        ident_b = cstb[:, CB_ID:CB_ID + 128]
        U = cst[:, C_U:C_U + 128]
        ctk = tuple(self.cst.tks)
        v3 = lambda ap: ap.rearrange("p (s h) -> p s h", s=4)
        _ndt = 99
        _c = [0]
        def _ok():
            _c[0] += 1
            return _c[0] <= _ndt
        if not self.wdt_loaded and not P.dry:
            self.ensure_region(2)
            P.dma("sp", self.wdt.t[:, :], self.wscr[BDT, :, 0:256], reads=(self.region_tk[2],), writes=self.wdt.tks)
            self.wdt_loaded = True
        bk, btk = self.nb()
        if _ok():
            P.op("pe", [MM(bk[:, s * 32:(s + 1) * 32], h1T.t[:, kc, s * 128:(s + 1) * 128], self.wdt.t[:, kc * 32:(kc + 1) * 32], kc == 0, kc == 7)
                        for s in range(4) for kc in range(8)], reads=tuple(h1T.tks) + tuple(self.wdt.tks), writes=(btk,))
        dtu, dtv, da, acum, eac, edec, wdf, wex = [dtt[n] for n in ("dtu", "dtv", "da", "acum", "eac", "edec", "wdf", "wex")]
        if _ok():
            P.op("dve", TT(dtu.t[:, :, :], v3(bk[:, 0:128]), small[:, SC_DTB:SC_DTB + 32].unsqueeze(1).to_broadcast([128, 4, 32]), ALU.add),
                 reads=(btk,) + tuple(self.small.tks), writes=dtu.tks)
        if _ok():
            P.op("act", ACTF(dtv.t[:, :, :], dtu.t[:, :, :], AF.Exp), reads=dtu.tks, writes=dtv.tks)
        if _ok():
            P.op("act", ACTF(dtv.t[:, :, :], dtv.t[:, :, :], AF.Ln, bias=1.0), reads=dtv.tks, writes=dtv.tks)
        if _ok():
            P.op("dve", TT(da.t[:, :, :], dtv.t[:, :, :], self.abc.t[:, :].unsqueeze(1).to_broadcast([128, 4, 32]), ALU.mult),
                 reads=tuple(dtv.tks) + tuple(self.abc.tks), writes=da.tks)
        bA, bAtk = self.nb()
        if _ok():
            P.op("pe", [MM(bA[:, s * 32:(s + 1) * 32], U, da.t[:, s, :]) for s in range(4)], reads=ctk + tuple(da.tks), writes=(bAtk,))
        if _ok():
            P.op("act", ACTF(eac.t[:, :, :], v3(bA[:, 0:128]), AF.Exp), reads=(bAtk,), writes=eac.tks)
        if _ok():
            P.op("dve", CP(acum.t[:, :, :], v3(bA[:, 0:128])), reads=(bAtk,), writes=acum.tks)
        bL, bLtk = self.nb()
        if _ok():
            P.op("pe", [MM(bL[:, s * 32:(s + 1) * 32], cst[:, C_ONE:C_ONE + 128], da.t[:, s, :]) for s in range(4)], reads=ctk + tuple(da.tks), writes=(bLtk,))
        if _ok():
            P.op("act", ACTF(edec.t[:, :, :], v3(bL[:, 0:128]), AF.Exp), reads=(bLtk,), writes=edec.tks)
        if _ok():
            P.op("dve", TT(wdf.t[:, :, :], v3(bL[:, 0:128]), acum.t[:, :, :], ALU.subtract), reads=(bLtk,) + tuple(acum.tks), writes=wdf.tks)
        if _ok():
            P.op("act", ACTF(wex.t[:, :, :], wdf.t[:, :, :], AF.Exp), reads=wdf.tks, writes=wex.tks)
        b4 = lambda buf, c, g, n: buf.t[:, c, 4 * g:4 * g + 4].unsqueeze(2).to_broadcast([128, 4, n])
        if self.ckpt(1):
            return
        raw, xc, sz, st, carry, ynT = self.raw, self.xc, self.sz, self.st, self.carry, self.ynT
        xcs, szs = [self.xc, self.xc2], [self.sz, self.sz2]

        def phaseA(g):
            xc, sz = xcs[g % 2], szs[g % 2]
            zx, zxtk = self.W.get(BIN + 24 + 6 * g, 4)
            bc_, bctk = self.W.get(BIN + 24 + 6 * g + 4, 2)
            P.op("pool", CP(raw.t[:, :, 0:3], carry.t[:, 4 * g:4 * g + 4, :]), reads=(carry.tks[g],), writes=raw.tks)
            srcs = [(zx, zxtk, 2048, 2 * g), (zx, zxtk, 3072, 2 * g + 1), (bc_, bctk, 0, 16 + g), (bc_, bctk, 1024, 24 + g)]

            def conv_chain(i, sl, sltk, o0, cch):
                bk, btk = self.nb()
                P.op("pe", [MM(bk[:, :], sl[:, o0 + kc * 128:o0 + (kc + 1) * 128], h1T.t[:, kc, :], kc == 0, kc == 7) for kc in range(8)],
                     reads=(sltk,) + tuple(h1T.tks), writes=(btk,))
                yield
                P.op("act", ACTF(raw.t[:, i, 3:515], bk[:, :], AF.Copy), reads=(btk,), writes=(raw.tks[i],))
                yield
                acc = self.mt[i]
                cw = lambda k: small[:, SC_CW + cch * 4 + k:SC_CW + cch * 4 + k + 1]
                P.op("act", ACTF(acc.t[:, :], raw.t[:, i, 0:512], AF.Identity, bias=small[:, SC_CB + cch:SC_CB + cch + 1], scale=cw(0)),
                     reads=(raw.tks[i],) + tuple(self.small.tks), writes=acc.tks)
                yield
                for k in range(1, 4):
                    P.op("dve", STT(acc.t[:, :], raw.t[:, i, k:k + 512], cw(k), acc.t[:, :], ALU.mult, ALU.add), reads=(raw.tks[i],) + tuple(acc.tks), writes=acc.tks)
                    yield
                P.op("act", ACTF(xc.t[:, i, :], acc.t[:, :], AF.Silu), reads=acc.tks, writes=(xc.tks[i],))
                yield

            def z_chain(pair):
                bz, bztk = self.nb()
                for s_ in (2 * pair, 2 * pair + 1):
                    P.op("pe", [MM(bz[:, (s_ % 2) * 256:(s_ % 2 + 1) * 256], h1T.t[:, kc, s_ * 128:(s_ + 1) * 128], zx[:, kc * 256:(kc + 1) * 256], kc == 0, kc == 7) for kc in range(8)],
                         reads=(zxtk, h1T.tks[s_]), writes=(bztk,))
                yield
                yield
                yield
                yield
                yield
                yield
                for s_ in (2 * pair, 2 * pair + 1):
                    P.op("act", ACTF(sz.t[:, s_, :], bz[:, (s_ % 2) * 256:(s_ % 2 + 1) * 256], AF.Silu), reads=(bztk,), writes=(sz.tks[s_],))
                yield

            def tail():
                for _ in range(8):
                    yield
                P.op("pool", CP(carry.t[:, 4 * g:4 * g + 4, :], raw.t[:, :, 512:515]), reads=raw.tks, writes=(carry.tks[g],))
                yield
            return [conv_chain(i, *srcs[i]) for i in range(4)] + [z_chain(0), z_chain(1), tail()]

        interleave(phaseA(0))
        for g in range(NG):
            xc, sz = xcs[g % 2], szs[g % 2]
            v4 = lambda ap: ap.rearrange("p (j q) -> p j q", j=4)
            bIs = [None] * 4
            bYs = [None] * 4

            def phaseB(c):
                k = self.ck[c]
                cs = slice(c * 128, (c + 1) * 128)
                bT, bTtk = self.nb()
                bTb = bT[:, :].bitcast(BF16)
                P.op("pe", [TR(bTb[:, 0:128], xc.t[:, 0, cs], ident_b), TR(bTb[:, 128:256], xc.t[:, 1, cs], ident_b), TR(bTb[:, 256:384], xc.t[:, 2, cs], ident_b)],
                     reads=tuple(xc.tks[0:3]) + tuple(self.cstb.tks), writes=(bTtk,))
                yield
                P.op("act", ACTF(k["xtok"].t[:, 0:384], bTb[:, 0:384], AF.Copy), reads=(bTtk,), writes=k["xtok"].tks)
                P.op("dve", TT(v4(k["xdt"].t[:, :]), v4(bTb[:, 0:256]), b4(dtv, c, g, 64), ALU.mult), reads=(bTtk,) + tuple(dtv.tks), writes=k["xdt"].tks)
                P.op("pool", TT(v4(k["R1"].t[:, :]), U.unsqueeze(1).to_broadcast([128, 4, 128]), b4(da, c, g, 128), ALU.mult), reads=ctk + tuple(da.tks), writes=k["R1"].tks)
                yield
                bCB, bCBtk = self.nb()
                P.op("pe", [MM(bCB[:, 0:128], xc.t[:, 2, cs], xc.t[:, 3, cs])], reads=(xc.tks[2], xc.tks[3]), writes=(bCBtk,))
                bD, bDtk = self.nb()
                P.op("pe", [MM(bD[:, :], cst[:, C_SL:C_SL + 128], k["R1"].t[:, :])], reads=ctk + tuple(k["R1"].tks), writes=(bDtk,))
                yield
                P.op("pool", TT(v4(k["xw"].t[:, :]), v4(k["xdt"].t[:, :]), b4(wex, c, g, 64), ALU.mult), reads=tuple(k["xdt"].tks) + tuple(wex.tks), writes=k["xw"].tks)
                P.op("dve", TT(k["CBm"].t[:, :], bCB[:, 0:128], U, ALU.mult), reads=(bCBtk,) + ctk, writes=k["CBm"].tks)
                P.op("act", ACTF(k["L"].t[:, :], bD[:, :], AF.Exp), reads=(bDtk,), writes=k["L"].tks)
                yield
                bI, bItk = self.nb(hold=True)
                P.op("pe", [MM(bI[:, 0:256], k["xtok"].t[:, 256:384], k["xw"].t[:, :])], reads=tuple(k["xtok"].tks) + tuple(k["xw"].tks), writes=(bItk,))
                bIs[c] = (bI, bItk)
                P.op("dve", TT(v4(k["MT"].t[:, :]), v4(k["L"].t[:, :]), k["CBm"].t[:, :].unsqueeze(1).to_broadcast([128, 4, 128]), ALU.mult),
                     reads=tuple(k["L"].tks) + tuple(k["CBm"].tks), writes=k["MT"].tks)
                P.op("pool", TT(v4(k["xd"].t[:, :]), v4(k["xtok"].t[:, 0:256]), small[:, SC_DSK + 4 * g:SC_DSK + 4 * g + 4].unsqueeze(2).to_broadcast([128, 4, 64]), ALU.mult),
                     reads=tuple(k["xtok"].tks) + tuple(self.small.tks), writes=k["xd"].tks)
                yield
                bY, bYtk = self.nb(hold=True)
                bYs[c] = (bY, bYtk)
                P.op("pe", [MM(bY[:, j * 64:(j + 1) * 64], k["MT"].t[:, j * 128:(j + 1) * 128], k["xdt"].t[:, j * 64:(j + 1) * 64]) for j in range(4)],
                     reads=tuple(k["MT"].tks) + tuple(k["xdt"].tks), writes=(bYtk,))
                yield

            def phaseD(c):
                k = self.ck[c]
                cs = slice(c * 128, (c + 1) * 128)
                bY, bYtk = bYs[c]
                P.op("pe", [MM(bY[:, 256:512], xc.t[:, 3, cs], self.stbc.t[:, c, :])], reads=(xc.tks[3],) + tuple(self.stbc.tks), writes=(bYtk,))
                yield
                t1 = k["t1"]
                P.op("dve", TT(v4(t1.t[:, :]), v4(bY[:, 256:512]), b4(eac, c, g, 64), ALU.mult), reads=(bYtk,) + tuple(eac.tks), writes=t1.tks)
                P.op("dve", TT(t1.t[:, :], t1.t[:, :], bY[:, 0:256], ALU.add), reads=tuple(t1.tks) + (bYtk,), writes=t1.tks)
                self.unhold(bYtk)
                yield
                P.op("pool", TT(t1.t[:, :], t1.t[:, :], k["xd"].t[:, :], ALU.add), reads=tuple(t1.tks) + tuple(k["xd"].tks), writes=t1.tks)
                P.op("pool", TT(t1.t[:, :], t1.t[:, :], sz.t[:, c, :], ALU.mult), reads=tuple(t1.tks) + (sz.tks[c],), writes=t1.tks)
                yield
                ss = k["ss"]
                P.op("act", ACTF(k["sq"].t[:, 0:256], t1.t[:, :], AF.Square, accum_out=ss.t[:, 0:1]), reads=t1.tks, writes=tuple(k["sq"].tks) + tuple(ss.tks))
                yield
                P.op("act", ACTF(ss.t[:, 1:2], ss.t[:, 0:1], AF.Ln, bias=cst[:, C_NH + 1:C_NH + 2], scale=1.0 / 256.0), reads=tuple(ss.tks) + ctk, writes=ss.tks)
                P.op("act", ACTF(ss.t[:, 2:3], ss.t[:, 1:2], AF.Exp, scale=-0.5), reads=ss.tks, writes=ss.tks)
                yield
                P.op("dve", TSC(k["yn"].t[:, :], t1.t[:, :], ss.t[:, 2:3], None, ALU.mult), reads=tuple(t1.tks) + tuple(ss.tks), writes=k["yn"].tks)
                yield

            def phaseD2(c):
                k = self.ck[c]
                cs = slice(c * 128, (c + 1) * 128)
                bT2, bT2tk = self.nb()
                b2 = bT2[:, :].bitcast(BF16)
                P.op("pe", [TR(b2[:, 0:128], k["yn"].t[:, 0:128], ident_b), TR(b2[:, 128:256], k["yn"].t[:, 128:256], ident_b)],
                     reads=tuple(k["yn"].tks) + tuple(self.cstb.tks), writes=(bT2tk,))
                yield
                for i2 in range(2):
                    P.op("act", ACTF(ynT.t[:, 2 * g + i2, cs], b2[:, i2 * 128:(i2 + 1) * 128], AF.Identity, scale=small[:, SC_NW + 2 * g + i2:SC_NW + 2 * g + i2 + 1]),
                         reads=(bT2tk,) + tuple(self.small.tks), writes=(ynT.tks[2 * g + i2],))
                yield

            interleave([phaseB(c) for c in range(NSUB)])
            stg = st.t[:, g * 256:(g + 1) * 256]
            for c in range(NSUB):
                P.op("dve", CP(self.stbc.t[:, c, :], stg), reads=(st.tks[g],), writes=self.stbc.tks)
                P.op("dve", TT(v4(stg), v4(stg), b4(edec, c, g, 64), ALU.mult), reads=(st.tks[g],) + tuple(edec.tks), writes=(st.tks[g],))
                P.op("dve", TT(stg, stg, bIs[c][0][:, 0:256], ALU.add), reads=(st.tks[g], bIs[c][1]), writes=(st.tks[g],))
                self.unhold(bIs[c][1])
            interleave([phaseD(c) for c in range(NSUB)])
            if g + 1 < NG:
                interleave(phaseA(g + 1))
            interleave([phaseD2(c) for c in range(NSUB)])

    def attention(self, seg, t):
        P = self.P
        cst, cstb = self.cst.t, self.cstb.t
        QT, OT, biasT = self.QT, self.OT, self.biasT
        nkt = 4 * (t + 1)
        ones_b = cstb[:, CB_ONE:CB_ONE + 128]
        self.ring = [0, 1, 2, 3]
        def load_kv(h):
            kb, vb = self.kbuf[h % 2], self.vbuf[h % 2]
            P.dma("sp", kb.t[:, 0:nkt * 128], self.kt_s[h, :, 0:nkt * 128], reads=(self.kts_tk,), writes=kb.tks)
            P.dma("sp", vb.t[:, 0:nkt, :], self.v_s[0:nkt * 128, h * 128:(h + 1) * 128].rearrange("(k p) d -> p k d", p=128), reads=(self.vs_tk,), writes=vb.tks)

        load_kv(0)
        for h in range(8):
            kb, vb = self.kbuf[h % 2], self.vbuf[h % 2]
            if h + 1 < 8:
                load_kv(h + 1)
            io = 4 + 2 * (h % 2)
            bO, bOtk, bR, bRtk = self.banks[io], self.bank_tk[io], self.banks[io + 1], self.bank_tk[io + 1]

            def qk(kt):
                qlo = 0 if kt < 4 * t else (kt - 4 * t) * 128
                bS, bStk = self.nb()
                PT = self.PT[kt % 2]
                P.op("pe", [MM(bS[:, qlo:512], kb.t[:, kt * 128:(kt + 1) * 128], QT.t[:, h, qlo:512], True, False),
                            MM(bS[:, qlo:512], cstb[0:8, CB_OH + (kt // 2) * 128:CB_OH + (kt // 2 + 1) * 128], biasT.t[0:8, h, qlo:512], False, True)],
                     reads=tuple(kb.tks) + (QT.tks[h],) + tuple(biasT.tks) + tuple(self.cstb.tks), writes=(bStk,))
                P.op("act", ACTF(PT.t[:, qlo:512], bS[:, qlo:512], AF.Exp, scale=ATT_SCALE), reads=(bStk,), writes=PT.tks)
                if kt >= 4 * t:
                    P.op("pool", TT(PT.t[:, qlo:qlo + 128], PT.t[:, qlo:qlo + 128], cst[:, C_U:C_U + 128], ALU.mult), reads=tuple(PT.tks) + tuple(self.cst.tks), writes=PT.tks)
                return PT, qlo

            cur = qk(0)
            for kt in range(nkt):
                nxt = qk(kt + 1) if kt + 1 < nkt else None
                PT, qlo = cur
                P.op("pe", [MM(bO[:, qlo:512], vb.t[:, kt, :], PT.t[:, qlo:512], kt == 0, kt == nkt - 1),
                            MM(bR[:, qlo:512], ones_b, PT.t[:, qlo:512], kt == 0, kt == nkt - 1)],
                     reads=tuple(vb.tks) + tuple(PT.tks) + tuple(self.cstb.tks), writes=(bOtk, bRtk))
                cur = nxt
            P.op("dve", lambda e, bR=bR: e.reciprocal(out=self.rinv.t[:, :], in_=bR[:, :]), reads=(bRtk,), writes=self.rinv.tks)
            P.op("dve", TT(OT.t[:, h, :], bO[:, :], self.rinv.t[:, :], ALU.mult), reads=(bOtk,) + tuple(self.rinv.tks), writes=(OT.tks[h],))
        self.ring = list(range(8))

    def merge(self, seg, t, r0):
        P = self.P
        self.load_ln(1)
        small = self.small.t
        h1T, ynT, OT, mgT, mt = self.h1T, self.ynT, self.OT, self.mgT, self.mt
        for j in range(8):
            so, sotk = self.W.get(BSO + 2 * j, 2)
            b1, b1tk = self.nb()
            P.op("pe", [MM(b1[:, :], so[:, kc * 128:(kc + 1) * 128], ynT.t[:, kc, :], kc == 0, kc == 15) for kc in range(16)], reads=(sotk,) + tuple(ynT.tks), writes=(b1tk,))
            ao, aotk = self.W.get(BAO + j, 1)
            b2, b2tk = self.nb()
            P.op("pe", [MM(b2[:, :], ao[:, kc * 128:(kc + 1) * 128], OT.t[:, kc, :], kc == 0, kc == 7) for kc in range(8)], reads=(aotk,) + tuple(OT.tks), writes=(b2tk,))
            gt, gttk = self.W.get(BIN + 72 + 2 * j, 2)
            b3, b3tk = self.nb()
            P.op("pe", [MM(b3[:, :], gt[:, kc * 128:(kc + 1) * 128], h1T.t[:, kc, :], kc == 0, kc == 7) for kc in range(8)], reads=(gttk,) + tuple(h1T.tks), writes=(b3tk,))
            b4_, b4tk = self.nb()
            P.op("pe", [MM(b4_[:, :], gt[:, 1024 + kc * 128:1024 + (kc + 1) * 128], h1T.t[:, kc, :], kc == 0, kc == 7) for kc in range(8)], reads=(gttk,) + tuple(h1T.tks), writes=(b4tk,))
            P.op("act", ACTF(mt[0].t[:, :], b3[:, :], AF.Sigmoid, bias=small[:, SC_BG + j:SC_BG + j + 1]), reads=(b3tk,) + tuple(self.small.tks), writes=mt[0].tks)
            P.op("act", ACTF(mt[1].t[:, :], b4_[:, :], AF.Sigmoid, bias=small[:, SC_BG + 8 + j:SC_BG + 8 + j + 1]), reads=(b4tk,) + tuple(self.small.tks), writes=mt[1].tks)
            P.op("dve", TT(mt[2].t[:, :], mt[0].t[:, :], b1[:, :], ALU.mult), reads=tuple(mt[0].tks) + (b1tk,), writes=mt[2].tks)
            P.op("dve", TT(mt[3].t[:, :], mt[1].t[:, :], b2[:, :], ALU.mult), reads=tuple(mt[1].tks) + (b2tk,), writes=mt[3].tks)
            P.op("pool", TT(mgT.t[:, j, :], mt[2].t[:, :], mt[3].t[:, :], ALU.add), reads=tuple(mt[2].tks) + tuple(mt[3].tks), writes=(mgT.tks[j],))
        wo = [self.W.get(BWO, 4), self.W.get(BWO + 4, 4)]
        for s in range(NSUB):
            bk2 = [self.nb(), self.nb()]
            for h in range(2):
                P.op("pe", [MM(bk2[h][0][:, :], mgT.t[:, kc, s * 128:(s + 1) * 128], wo[h][0][:, kc * 512:(kc + 1) * 512], kc == 0, kc == 7) for kc in range(8)],
                     reads=tuple(mgT.tks) + (wo[h][1],), writes=(bk2[h][1],))
            self.ln_math(s, bk2, 1.0, None, r0, "h2")
        for s in range(NSUB):
            self.ln_post(s, self.ffnT)


def _fm_chunks(W, c0, ncc):
    K = W.shape[0]
    nk = K // 128
    sub = W[:, c0:c0 + ncc * 128].reshape(nk, 128, ncc, 128)
    return np.ascontiguousarray(sub.transpose(2, 1, 0, 3)).reshape(ncc, 128, nk * 128)


def _tm_block(W, c0, n):
    K = W.shape[0]
    nk = K // 128
    sub = W[:, c0:c0 + n].reshape(nk, 128, n)
    return np.ascontiguousarray(sub.transpose(1, 0, 2)).reshape(128, nk * n)


def pack_weights(inp):
    wp = np.zeros((NBLK, 128, 1024), np.float32)
    for fi, pre in enumerate(("ffn1", "ffn2")):
        wp[BG[fi]:BG[fi] + 22] = _fm_chunks(inp[pre + "_w_gate"], 0, 22)
        wp[BU[fi]:BU[fi] + 22] = _fm_chunks(inp[pre + "_w_up"], 0, 22)
        wp[BD[fi]:BD[fi] + 22] = inp[pre + "_w_down"].reshape(22, 128, 1024)
    win = inp["w_in"]
    wp[BIN + 0:BIN + 8] = _fm_chunks(win, OFF_Q, 8)
    wp[BIN + 8:BIN + 16] = _fm_chunks(win, OFF_K, 8)
    for vb in range(2):
        wp[BIN + 16 + 4 * vb:BIN + 20 + 4 * vb] = _tm_block(win, OFF_V + vb * 512, 512).reshape(128, 4, 1024).transpose(1, 0, 2)
    for g in range(NG):
        b = BIN + 24 + 6 * g
        wp[b:b + 2] = _tm_block(win, OFF_Z + g * 256, 256).reshape(128, 2, 1024).transpose(1, 0, 2)
        wp[b + 2:b + 4] = _fm_chunks(win, OFF_X + g * 256, 2)
        wp[b + 4:b + 5] = _fm_chunks(win, OFF_B + g * 128, 1)
        wp[b + 5:b + 6] = _fm_chunks(win, OFF_C + g * 128, 1)
    for j in range(8):
        wp[BIN + 72 + 2 * j] = _fm_chunks(win, OFF_G + j * 128, 1)[0]
        wp[BIN + 72 + 2 * j + 1] = _fm_chunks(win, OFF_G + 1024 + j * 128, 1)[0]
    wp[BDT, :, 0:256] = _tm_block(win, OFF_DT, 32)
    so = _fm_chunks(inp["w_ssm_out"], 0, 8)
    wp[BSO:BSO + 16] = so.reshape(8, 128, 2, 1024).transpose(0, 2, 1, 3).reshape(16, 128, 1024)
    wp[BAO:BAO + 8] = _fm_chunks(inp["w_att_out"], 0, 8)
    for hb in range(2):
        wp[BWO + 4 * hb:BWO + 4 * hb + 4] = _tm_block(inp["w_o"], hb * 512, 512).reshape(128, 4, 1024).transpose(1, 0, 2)
    return wp


def make_consts():
    c = np.zeros((128, CST_COLS), np.float32)
    i = np.arange(128)
    c[:, C_ID:C_ID + 128] = np.eye(128, dtype=np.float32)
    c[:, C_U:C_U + 128] = (i[:, None] <= i[None, :]).astype(np.float32)
    c[:, C_SL:C_SL + 128] = (i[:, None] > i[None, :]).astype(np.float32)
    c[:, C_ONE:C_ONE + 128] = 1.0
    qb = np.arange(8)[:, None]
    j = np.arange(8)[None, :]
    c[:, C_FUT:C_FUT + 64] = np.where(j >= qb, -1e30, 0.0).astype(np.float32).reshape(1, 64)
    c[:, C_VAL:C_VAL + 64] = (j < qb).astype(np.float32).reshape(1, 64)
    c[:, C_OWN:C_OWN + 64] = (j == qb).astype(np.float32).reshape(1, 64)
    oh = np.zeros((128, 8, 128), np.float32)
    for k in range(8):
        oh[k, k, :] = 1.0
    c[:, C_OH:C_OH + 1024] = oh.reshape(128, 1024)
    c[:, C_NH] = -0.5
    c[:, C_NH + 1] = RMS_EPS
    return c


def make_small(inp):
    s = np.zeros((128, SC_COLS), np.float32)
    s[:, SC_CW:SC_CW + 128] = inp["conv_w"].reshape(4, 32, 128).transpose(2, 1, 0).reshape(128, 128)
    s[:, SC_CB:SC_CB + 32] = inp["conv_b"].reshape(32, 128).T
    s[:, SC_NW:SC_NW + 16] = inp["ssm_norm_w"].reshape(16, 128).T
    s[:, SC_BG:SC_BG + 16] = inp["b_gate"].reshape(16, 128).T
    s[:, SC_DTB:SC_DTB + 32] = inp["dt_bias"][None, :]
    s[:, SC_ALOG:SC_ALOG + 32] = inp["a_log"][None, :]
    s[:, SC_DSK:SC_DSK + 32] = inp["d_skip"][None, :]
    return s


def make_lnp(inp):
    l = np.zeros((3, 128, 2 * D), np.float32)
    gains = [inp["ln1_g"], inp["ln2_g"], inp["ln3_g"]]
    biases = [inp["ln1_b"], inp["ln2_b"], inp["ln3_b"]]
    for i in range(3):
        l[i, :, 0:D] = gains[i][None, :]
        l[i, :, D:] = biases[i][None, :]
    return l


_CACHE = {}


def run(inputs, ncores=NCORES, nseq=None, nseg_limit=None, dbg=None):
    inp = {k: np.asarray(v, dtype=np.float32) for k, v in inputs.items()}
    x = inp["x"]
    if nseq is None:
        nseq = x.shape[0] // ncores
    key = (nseq, nseg_limit, dbg)
    if key not in _CACHE:
        _CACHE[key] = Builder(nseq, nseg_limit, dbg).build()
    nc = _CACHE[key]
    wp = pack_weights(inp)
    cst = make_consts()
    small = make_small(inp)
    lnp = make_lnp(inp)
    xs = x.reshape(-1, D)
    in_maps = []
    for c in range(ncores):
        in_maps.append({"x": np.ascontiguousarray(xs[c * nseq * SEQ:(c + 1) * nseq * SEQ]), "wpack": wp,
                        "cst_f32": cst, "smallc": small, "lnp": lnp})
    r = run_bass_kernel_spmd(nc, in_maps, core_ids=list(range(ncores)))
    out = np.concatenate([r.results[c]["out"] for c in range(ncores)], axis=0)
    if dbg:
        d = np.concatenate([r.results[c]["dbg_out"] for c in range(ncores)], axis=0)
        global LAST_DBGB
        LAST_DBGB = [r.results[c]["dbgb_out"] for c in range(ncores)]
        return out, d
    return out


def kernel(**inputs):
    out = run(inputs)
    return out.reshape(BATCH, SEQ, D).astype(np.float32)
```
